# Optimizing a Trainium2 kernel written in Bass

```python
import jax, jax.numpy as jnp
from jax import lax
import numpy as np

D_MODEL = 2048
BATCH = 8
SEQ = 4096
DEPTH = 4

N_A = DEPTH // 2
N_B = DEPTH - N_A
MIX_W = D_MODEL
MEM_HEADS = 4
MEM_HEAD_DIM = 128
MEM_W = MEM_HEADS * MEM_HEAD_DIM
CHUNK = 128
G_HEADS = 12
G_DIM = 128
G_W = G_HEADS * G_DIM
MLA_HEADS = 12
NOPE_DIM = 128
ROPE_DIM = 64
V_DIM = 128
Q_RANK = 512
KV_RANK = 512
Q_BLOCK = 128
ROPE_THETA = 10000.0
D_FF = 5632
CONV_W = 3
EPS = 1e-6

kernel_name = "yoco_gmlp_mla_memxattn_convffn"


def rmsnorm(x, g):
    x32 = x.astype(jnp.float32)
    y = x32 * lax.rsqrt(jnp.mean(x32 * x32, axis=-1, keepdims=True) + EPS)
    return (y * g.astype(jnp.float32)).astype(x.dtype)


def rope_tables(positions, dtype):
    inv = 1.0 / (ROPE_THETA ** (jnp.arange(0, ROPE_DIM, 2, dtype=jnp.float32) / ROPE_DIM))
    ang = positions.astype(jnp.float32)[..., None] * inv
    return jnp.cos(ang).astype(dtype), jnp.sin(ang).astype(dtype)


def apply_rope(x, cos, sin):
    x1, x2 = jnp.split(x, 2, axis=-1)
    return jnp.concatenate([x1 * cos - x2 * sin, x2 * cos + x1 * sin], axis=-1)


def spatial_gating(z_u, z_v, g_v, w_sp, b_sp):
    B, S, _ = z_u.shape
    u = jax.nn.gelu(z_u, approximate=False)
    v = rmsnorm(jax.nn.gelu(z_v, approximate=False), g_v)
    vb = v.reshape(B, S // CHUNK, CHUNK, G_HEADS, G_DIM)
    w = w_sp * jnp.tril(jnp.ones((CHUNK, CHUNK), w_sp.dtype))
    sv = jnp.einsum('gts,bnsgc->bntgc', w, vb) + b_sp.T[None, None, :, :, None]
    return u * sv.reshape(B, S, G_W)


def mla_attention(q_lat, g_q_lat, w_uq, w_uk, w_uv, c_kv, k_rope, cos, sin):
    B, S, _ = q_lat.shape
    q = (rmsnorm(q_lat, g_q_lat) @ w_uq).reshape(B, S, MLA_HEADS, NOPE_DIM + ROPE_DIM)
    q_nope = q[..., :NOPE_DIM]
    q_rope = apply_rope(q[..., NOPE_DIM:], cos[:, :, None, :], sin[:, :, None, :])
    n_blk = S // Q_BLOCK
    scale = (NOPE_DIM + ROPE_DIM) ** -0.5
    k_pos = jnp.arange(S)

    def to_blocks(t):
        return jnp.moveaxis(t.reshape(B, n_blk, Q_BLOCK, *t.shape[2:]), 1, 0)

    def block(args):
        qn, qr, i = args
        qa = jnp.einsum('bqhn,rhn->bqhr', qn, w_uk)
        s = (jnp.einsum('bqhr,bkr->bhqk', qa, c_kv)
             + jnp.einsum('bqhp,bkp->bhqk', qr, k_rope)).astype(jnp.float32) * scale
        q_pos = i * Q_BLOCK + jnp.arange(Q_BLOCK)
        s = jnp.where(k_pos[None, :] <= q_pos[:, None], s, -jnp.inf)
        p = jax.nn.softmax(s, axis=-1).astype(c_kv.dtype)
        o_lat = jnp.einsum('bhqk,bkr->bqhr', p, c_kv)
        return jnp.einsum('bqhr,rhv->bqhv', o_lat, w_uv)

    o = lax.map(block, (to_blocks(q_nope), to_blocks(q_rope), jnp.arange(n_blk)))
    return jnp.moveaxis(o, 0, 1).reshape(B, S, MLA_HEADS * V_DIM)


def memory_attention(q_m, mem, g_mem, w_mem_kv):
    B, S, _ = q_m.shape
    M = mem.shape[1]
    q = q_m.reshape(B, S, MEM_HEADS, MEM_HEAD_DIM)
    kv = rmsnorm(mem, g_mem) @ w_mem_kv
    k = kv[..., :MEM_W].reshape(B, M, MEM_HEADS, MEM_HEAD_DIM)
    v = kv[..., MEM_W:].reshape(B, M, MEM_HEADS, MEM_HEAD_DIM)
    s = jnp.einsum('bshd,bmhd->bhsm', q, k).astype(jnp.float32) * (MEM_HEAD_DIM ** -0.5)
    p = jax.nn.softmax(s, axis=-1).astype(v.dtype)
    return jnp.einsum('bhsm,bmhd->bshd', p, v).reshape(B, S, MEM_W)


def conv_ffn(h, w_up, cw, cb, w_down):
    S = h.shape[1]
    a = h @ w_up
    ap = jnp.pad(a, ((0, 0), (CONV_W - 1, 0), (0, 0)))
    c = sum(ap[:, k:k + S] * cw[k] for k in range(CONV_W)) + cb
    gate, val = c[..., :D_FF], c[..., D_FF:]
    return (jax.nn.silu(gate) * val) @ w_down


def setup_inputs(seed: int = 0) -> dict:
    key = jax.random.key(seed)
    ks = jax.random.split(key, 26)

    def nrm(k, shape, scale):
        return jax.random.normal(k, shape, jnp.float32) * scale

    def gain(k, shape):
        return 1.0 + 0.05 * jax.random.normal(k, shape, jnp.float32)

    D = D_MODEL
    return {
        "x": nrm(ks[0], (BATCH, SEQ, D), 1.0),
        "mem": nrm(ks[1], (BATCH, 256, D), 1.0),
        "positions": (jnp.arange(SEQ, dtype=jnp.int32)[None, :]
                      + jax.random.randint(ks[2], (BATCH, 1), 0, 1024, dtype=jnp.int32)),
        "g_mix": gain(ks[3], (DEPTH, D)),
        "g_ffn": gain(ks[4], (DEPTH, D)),
        "g_final": gain(ks[5], (D,)),
        "w_in_a": nrm(ks[6], (N_A, D, 2 * G_W + MEM_W), D ** -0.5),
        "g_v": gain(ks[7], (N_A, G_W)),
        "w_sp": nrm(ks[8], (N_A, G_HEADS, CHUNK, CHUNK), CHUNK ** -0.5),
        "b_sp": 1.0 + 0.1 * jax.random.normal(ks[9], (N_A, G_HEADS, CHUNK), jnp.float32),
        "g_kv": gain(ks[10], (D,)),
        "w_kv_a": nrm(ks[11], (D, KV_RANK + ROPE_DIM), D ** -0.5),
        "g_kv_lat": gain(ks[12], (KV_RANK,)),
        "w_in_b": nrm(ks[13], (N_B, D, Q_RANK + MEM_W), D ** -0.5),
        "g_q_lat": gain(ks[14], (N_B, Q_RANK)),
        "w_uq": nrm(ks[15], (N_B, Q_RANK, MLA_HEADS * (NOPE_DIM + ROPE_DIM)), Q_RANK ** -0.5),
        "w_uk": nrm(ks[16], (N_B, KV_RANK, MLA_HEADS, NOPE_DIM), KV_RANK ** -0.5),
        "w_uv": nrm(ks[17], (N_B, KV_RANK, MLA_HEADS, V_DIM), KV_RANK ** -0.5),
        "g_mem": gain(ks[18], (DEPTH, D)),
        "w_mem_kv": nrm(ks[19], (DEPTH, D, 2 * MEM_W), D ** -0.5),
        "w_out": nrm(ks[20], (DEPTH, MIX_W, D), MIX_W ** -0.5),
        "w_ffn_up": nrm(ks[21], (DEPTH, D, 2 * D_FF), D ** -0.5),
        "conv_w": nrm(ks[22], (DEPTH, CONV_W, 2 * D_FF), CONV_W ** -0.5),
        "conv_b": nrm(ks[23], (DEPTH, 2 * D_FF), 0.01),
        "w_ffn_down": nrm(ks[24], (DEPTH, D_FF, D), D_FF ** -0.5),
    }


def reference(x, mem, positions, g_mix, g_ffn, g_final, w_in_a, g_v, w_sp, b_sp,
              g_kv, w_kv_a, g_kv_lat, w_in_b, g_q_lat, w_uq, w_uk, w_uv,
              g_mem, w_mem_kv, w_out, w_ffn_up, conv_w, conv_b, w_ffn_down):
    cos, sin = rope_tables(positions, x.dtype)
    c_kv = None
    k_rope = None
    for l in range(DEPTH):
        if l == N_A:
            kv = rmsnorm(x, g_kv) @ w_kv_a
            c_kv = rmsnorm(kv[..., :KV_RANK], g_kv_lat)
            k_rope = apply_rope(kv[..., KV_RANK:], cos, sin)
        h = rmsnorm(x, g_mix[l])
        if l < N_A:
            z = h @ w_in_a[l]
            main = spatial_gating(z[..., :G_W], z[..., G_W:2 * G_W], g_v[l], w_sp[l], b_sp[l])
            q_m = z[..., 2 * G_W:]
        else:
            j = l - N_A
            z = h @ w_in_b[j]
            main = mla_attention(z[..., :Q_RANK], g_q_lat[j], w_uq[j], w_uk[j], w_uv[j],
                                 c_kv, k_rope, cos, sin)
            q_m = z[..., Q_RANK:]
        mo = memory_attention(q_m, mem, g_mem[l], w_mem_kv[l])
        x = x + jnp.concatenate([main, mo], axis=-1) @ w_out[l]
        x = x + conv_ffn(rmsnorm(x, g_ffn[l]), w_ffn_up[l], conv_w[l], conv_b[l], w_ffn_down[l])
    return rmsnorm(x, g_final)
```

```python
import contextlib
import numpy as np
import concourse.bass as bass
import concourse.mybir as mybir
from concourse.bass_utils import run_bass_kernel_spmd

F32 = mybir.dt.float32
BF16 = mybir.dt.bfloat16
I32 = mybir.dt.int32
ALU = mybir.AluOpType
AF = mybir.ActivationFunctionType

D = 2048
DFF = 5632
NFC = 44
TB = 512
EPS = 1e-6
ENGS = ("pe", "act", "dve", "pool", "sp")
EPOCH = 30000

GMIX, GFFN, GMEM, GFIN, GKV, GKVL, GQL, GV = 0, 64, 128, 192, 208, 224, 228, 236
CW = 260
CB = CW + 4 * 3 * 88
INV = CB + 4 * 88
SGN = INV + 1
EPSC = INV + 2
NC_ = INV + 4
PI_SAFE = 3.1415925
C1 = 6.28125
C2 = 2.0 * np.pi - 6.28125


class Tl:
    __slots__ = ("w", "r")

    def __init__(self):
        self.w = {}
        self.r = {}


class Op:
    __slots__ = ("eng", "fn", "deps", "needed", "sem", "val", "key", "stream")


class Prog:
    def __init__(self, nc, stack):
        self.nc = nc
        self.stack = stack
        self.ops = {e: [] for e in ENGS}

    def op(self, eng, fn, reads=(), writes=(), key=None, acc=False, extra=()):
        o = Op()
        o.eng = eng
        o.fn = fn
        o.key = key
        o.needed = False
        o.sem = None
        o.val = 0
        st = o.stream = key if key is not None else eng
        deps = set(extra)
        for t in reads:
            for s, p in t.w.items():
                if s == st and key is not None:
                    continue
                deps.add(p)
        for t in writes:
            for s, p in t.w.items():
                if s == st and (key is not None or acc):
                    continue
                deps.add(p)
            for s, p in t.r.items():
                if s == st and key is not None:
                    continue
                deps.add(p)
        for p in deps:
            p.needed = True
        o.deps = deps
        for t in reads:
            t.r[st] = o
        for t in writes:
            t.w = {st: o}
            t.r = {}
        self.ops[eng].append(o)
        return o

    def wait(self, eng, ops):
        return self.op(eng, None, extra=[o for o in ops if o is not None])

    def emit(self, block):
        nc, stack = self.nc, self.stack
        keysem, keycnt = {}, {}
        for e in ENGS:
            cnt, sem, nsem = 0, None, 0
            for o in self.ops[e]:
                if o.key is not None:
                    if o.key not in keysem:
                        keysem[o.key] = stack.enter_context(nc.semaphore("k_" + str(o.key)))
                        keycnt[o.key] = 0
                    keycnt[o.key] += 1
                    o.sem = keysem[o.key]
                    o.val = 16 * keycnt[o.key]
                elif o.needed and o.fn is not None:
                    if sem is None or cnt >= EPOCH:
                        sem = stack.enter_context(nc.semaphore("e_%s_%d" % (e, nsem)))
                        nsem += 1
                        cnt = 0
                    cnt += 1
                    o.sem = sem
                    o.val = cnt

        def run(e, eng):
            seen = {}
            for o in self.ops[e]:
                waits = {}
                for p in o.deps:
                    sid = id(p.sem)
                    if seen.get(sid, 0) >= p.val:
                        continue
                    if sid not in waits or waits[sid][1] < p.val:
                        waits[sid] = (p.sem, p.val)
                wl = list(waits.values())
                for s, v in wl:
                    seen[id(s)] = v
                if o.fn is None:
                    for s, v in wl:
                        eng.wait_ge(s, v)
                    continue
                for s, v in wl[1:]:
                    eng.wait_ge(s, v)
                ins = o.fn(eng)
                if wl:
                    ins._wait_ge(wl[0][0], wl[0][1])
                if o.key is not None:
                    ins.then_inc(o.sem, 16)
                elif o.sem is not None:
                    ins.then_inc(o.sem, 1)

        block.tensor(lambda eng: run("pe", eng))
        block.scalar(lambda eng: run("act", eng))
        block.vector(lambda eng: run("dve", eng))
        block.gpsimd(lambda eng: run("pool", eng))
        block.sync(lambda eng: run("sp", eng))


def wcat(ltypes):
    cat = {}
    for (ty, l, j) in ltypes:
        if ty == "A":
            def ina(i, l=l):
                if i < 6:
                    c0 = 1536 + 256 * i
                elif i < 12:
                    c0 = 256 * (i - 6)
                else:
                    c0 = 3072 + 256 * (i - 12)
                return [("w_in_a%d" % l, 0, c0, 256, 0)]
            cat["ina%d" % l] = (14, 16, 256, ina)
        else:
            cat["inb%d" % j] = (4, 16, 256, lambda i, j=j: [("w_in_b%d" % j, 0, 256 * i, 256, 0)])
            for nm in ("uqn", "uqr", "uk", "uv"):
                cat["%s%d" % (nm, j)] = (3, 4, 512, lambda i, j=j, nm=nm: [("w_%s%d" % (nm, j), 0, 512 * i, 512, 0)])
        cat["out%d" % l] = (8, 16, 256, lambda i, l=l: [("w_out%d" % l, 0, 256 * i, 256, 0)])
        cat["up%d" % l] = (NFC, 16, 256, lambda i, l=l: [("w_up%d" % l, 0, 128 * i, 128, 0),
                                                         ("w_up%d" % l, 0, DFF + 128 * i, 128, 128)])
        cat["down%d" % l] = (20, 8, 512, lambda i, l=l: [("w_down%d" % l, 8 * (i // 4), 512 * (i % 4), 512, 0)])
        cat["downt%d" % l] = (4, 4, 512, lambda i, l=l: [("w_down%d" % l, 40, 512 * i, 512, 0)])
        cat["mkv%d" % l] = (4, 16, 256, lambda i, l=l: [("w_mkv%d" % l, 0, 256 * i, 256, 0)])
    if any(t[0] == "B" for t in ltypes):
        cat["kva"] = (5, 16, 128, lambda i: [("w_kva", 0, 128 * i, 128, 0)])
    return cat


def wshapes(ltypes):
    sh = {}
    for (ty, l, j) in ltypes:
        if ty == "A":
            sh["w_in_a%d" % l] = (D, 3584)
            sh["w_spt%d" % l] = (128, 1536)
        else:
            sh["w_in_b%d" % j] = (D, 1024)
            for nm in ("uqn", "uqr", "uk", "uv"):
                sh["w_%s%d" % (nm, j)] = (512, 1536)
        sh["w_out%d" % l] = (D, D)
        sh["w_up%d" % l] = (D, 2 * DFF)
        sh["w_down%d" % l] = (DFF, D)
        sh["w_mkv%d" % l] = (D, 1024)
    if any(t[0] == "B" for t in ltypes):
        sh["w_kva"] = (D, 640)
    return sh


def build(S, ltypes):
    NB = S // TB
    nA = sum(1 for t in ltypes if t[0] == "A")
    hasB = any(t[0] == "B" for t in ltypes)
    nc = bass.Bass("TRN2", target_bir_lowering=False)
    dr = {}
    dr["x"] = nc.dram_tensor("x", [S, D], F32, kind="ExternalInput").ap()
    dr["mem"] = nc.dram_tensor("mem", [256, D], F32, kind="ExternalInput").ap()
    dr["posb"] = nc.dram_tensor("posb", [64, S], I32, kind="ExternalInput").ap()
    dr["cst"] = nc.dram_tensor("cst", [128, NC_], F32, kind="ExternalInput").ap()
    dr["cmat"] = nc.dram_tensor("cmat", [128, 256], F32, kind="ExternalInput").ap()
    dr["bsp"] = nc.dram_tensor("bsp", [128, max(nA, 1) * 1536], F32, kind="ExternalInput").ap()
    for nm, shp in wshapes(ltypes).items():
        dr[nm] = nc.dram_tensor(nm, list(shp), F32, kind="ExternalInput").ap()
    y = nc.dram_tensor("y", [S, D], F32, kind="ExternalOutput").ap()
    cat = wcat(ltypes)
    scr = {}
    for nm, (nt, kc, ct, _) in cat.items():
        scr[nm] = nc.dram_tensor("scr_" + nm, [nt, 128, kc * ct], BF16).ap()
    for (ty, l, j) in ltypes:
        if ty == "A":
            scr["sp%d" % l] = nc.dram_tensor("scr_sp%d" % l, [1, 128, 1536], BF16).ap()
    scr_mk = nc.dram_tensor("scr_mk", [max(1, len(ltypes)), 128, 2048], BF16).ap()

    with contextlib.ExitStack() as st:
        P = Prog(nc, st)
        NS = 6
        stg = [st.enter_context(nc.sbuf_tensor("stg%d" % i, [128, 4096], F32)) for i in range(NS)]
        cvt = [st.enter_context(nc.sbuf_tensor("cvt%d" % i, [128, 4096], BF16)) for i in range(NS)]
        trif = st.enter_context(nc.sbuf_tensor("trif", [128, 128], F32))
        Ttri = Tl()
        Ts = [Tl() for _ in range(NS)]
        Tc = [Tl() for _ in range(NS)]
        block = st.enter_context(nc.Block())
        P.op("sp", lambda e: e.dma_start(out=trif[:], in_=dr["cmat"][:, 128:256]), writes=[Ttri], key="tri")
        cnt = [0]
        stores = []

        def convert(dst_ap, n, pieces, mask=False):
            i = cnt[0] % NS
            cnt[0] += 1
            for (dst_v, src_v) in pieces(stg[i]):
                P.op("sp", (lambda d, s: lambda e: e.dma_start(out=d, in_=s))(dst_v, src_v), writes=[Ts[i]], key="ps%d" % i)
            if mask:
                a = stg[i][:, 0:n].rearrange("p (g t) -> p g t", t=128)
                o = cvt[i][:, 0:n].rearrange("p (g t) -> p g t", t=128)
                for g in range(12):
                    P.op("dve", (lambda o, a: lambda e: e.tensor_tensor(out=o, in0=a, in1=trif[:], op=ALU.mult))(o[:, g, :], a[:, g, :]),
                         reads=[Ts[i], Ttri], writes=[Tc[i]])
            elif cnt[0] % 3 == 0:
                P.op("act", (lambda i: lambda e: e.activation(cvt[i][:, 0:n], stg[i][:, 0:n], AF.Copy))(i), reads=[Ts[i]], writes=[Tc[i]])
            else:
                P.op("dve", (lambda i: lambda e: e.tensor_copy(cvt[i][:, 0:n], stg[i][:, 0:n]))(i), reads=[Ts[i]], writes=[Tc[i]])
            stores.append(P.op("pool", (lambda i: lambda e: e.dma_start(out=dst_ap, in_=cvt[i][:, 0:n]))(i), reads=[Tc[i]], key="pc%d" % i))

        for nm, (nt, kc, ct, pf) in cat.items():
            for ti in range(nt):
                def pieces(stg_t, ti=ti, kc=kc, ct=ct, pf=pf):
                    res = []
                    sv = stg_t[:, 0:kc * ct].rearrange("p (k c) -> p k c", c=ct)
                    for (src, r0, c0, ncol, d0) in pf(ti):
                        srcv = dr[src].rearrange("(k p) n -> p k n", p=128)[:, r0:r0 + kc, c0:c0 + ncol]
                        res.append((sv[:, :, d0:d0 + ncol], srcv))
                    return res
                convert(scr[nm][ti], kc * ct, pieces)
        for (ty, l, j) in ltypes:
            if ty == "A":
                convert(scr["sp%d" % l][0], 1536, (lambda l: lambda stg_t: [(stg_t[:, 0:1536], dr["w_spt%d" % l][:, :])])(l), mask=True)
        P.wait("pool", stores[-2 * NS:])
        P.emit(block)

    with contextlib.ExitStack() as st:
        P = Prog(nc, st)

        def sb(name, shape, dt):
            return st.enter_context(nc.sbuf_tensor(name, shape, dt))

        xT = sb("xT", [128, 16, TB], F32)
        TxT = [Tl() for _ in range(16)]
        hT = sb("hT", [128, 16, TB], BF16)
        ThT = [Tl() for _ in range(16)]
        catT = sb("catT", [128, 16, TB], BF16)
        Tcat = [Tl() for _ in range(16)]
        reg2 = sb("reg2", [128, 12 * TB], BF16)
        Treg2 = [Tl() for _ in range(12)]
        reg2c = reg2[:, :].rearrange("p (c t) -> p c t", t=TB)
        reg2v = reg2[:, :].rearrange("p (a v) -> p a v", v=1536)
        if hasB:
            ckvT = sb("ckvT", [128, 4, S], BF16)
            TckvT = [[Tl() for _ in range(NB)] for _ in range(4)]
            kropeT = sb("kropeT", [64, S], BF16)
            Tkrope = [Tl() for _ in range(NB)]
        nL = len(ltypes)
        mkb = sb("mkb", [128, 2048], BF16)
        Tmk = Tl()
        qlnb = sb("qlnb", [128, 4, TB], BF16)
        Tqln = [Tl() for _ in range(4)]
        rstd_fin = sb("rstd_fin", [128, TB], F32)
        Trfin = Tl()
        wspb = sb("wspb", [128, 1536], BF16)
        Twsp = Tl()
        NWS = 3
        wsl = [sb("wsl%d" % i, [128, 4096], BF16) for i in range(NWS)]
        Twsl = [Tl() for _ in range(NWS)]
        cstt = sb("cstt", [128, NC_], F32)
        ident = sb("ident", [128, 128], F32)
        trif = sb("trif2", [128, 128], F32)
        trib = sb("trib", [128, 128], BF16)
        ones_bf = sb("ones_bf", [128, 128], BF16)
        on2048 = sb("on2048", [128, 128], BF16)
        on512 = sb("on512", [128, 128], BF16)
        bspt = sb("bspt", [128, 1536], F32)
        Tbsp = Tl()
        halo = sb("halo", [128, nL * 88 * 2], F32)
        Thalo = [Tl() for _ in range(max(nL, 1) * 88)]
        Tk = Tl()
        NTF = 8
        tfs = [sb("tf%d" % i, [128, TB], F32) for i in range(NTF)]
        Ttf = [Tl() for _ in range(NTF)]
        NTBF = 7
        tbs = [sb("tb%d" % i, [128, TB], BF16) for i in range(NTBF)]
        Ttb = [Tl() for _ in range(NTBF)]
        NAE = 4
        aes = [sb("ae%d" % i, [128, TB + 2], F32) for i in range(NAE)]
        Tae = [Tl() for _ in range(NAE)]
        smalls = sb("smalls", [128, 16], F32)
        Tsm = [Tl() for _ in range(4)]
        posi = sb("posi", [64, TB], I32)
        Tposi = Tl()
        cos2 = sb("cos2", [64, TB], F32)
        sin2 = sb("sin2", [64, TB], F32)
        Tcos, Tsin = Tl(), Tl()
        banks = [st.enter_context(nc.psum_tensor("bank%d" % i, [128, TB], F32)) for i in range(8)]
        Tb = [Tl() for _ in range(8)]
        block = st.enter_context(nc.Block())

        rr = {"ps": 0, "tf": 0, "tb": 0, "ae": 0, "w": 0, "ev": 0}
        pinned = set()

        def ps_alloc():
            while True:
                i = rr["ps"] % 8
                rr["ps"] += 1
                if i not in pinned:
                    return i

        def tf_alloc():
            i = rr["tf"] % NTF
            rr["tf"] += 1
            rr["tfi"] = i
            return tfs[i], Ttf[i]

        def tb_alloc():
            i = rr["tb"] % NTBF
            rr["tb"] += 1
            return tbs[i], Ttb[i]

        def ae_alloc():
            i = rr["ae"] % NAE
            rr["ae"] += 1
            return aes[i], Tae[i]

        def cc(i):
            return cstt[:, i:i + 1]

        def mm(out, lhsT, rhs, start, stop, reads, writes):
            P.op("pe", lambda e: e.matmul(out, lhsT, rhs, start=start, stop=stop), reads=reads, writes=writes, acc=True)

        def tr(out, in_, reads, writes):
            P.op("pe", lambda e: e.transpose(out, in_, ident[:]), reads=list(reads) + [Tk], writes=writes, acc=True)

        def act(out, in_, func, reads, writes, bias=None, scale=1.0, accum_out=None):
            kw = {}
            if bias is not None:
                kw["bias"] = bias
            if accum_out is not None:
                kw["accum_out"] = accum_out
            P.op("act", lambda e: e.activation(out, in_, func, scale=scale, **kw), reads=reads, writes=writes)

        def tcopy(eng, out, in_, reads, writes):
            P.op(eng, lambda e: e.tensor_copy(out, in_), reads=reads, writes=writes)

        def evac(out, in_, reads, writes):
            rr["ev"] += 1
            if rr["ev"] % 2 == 0:
                act(out, in_, AF.Copy, reads, writes)
            else:
                tcopy("dve", out, in_, reads, writes)

        def tt(eng, out, in0, in1, op, reads, writes):
            P.op(eng, lambda e: e.tensor_tensor(out=out, in0=in0, in1=in1, op=op), reads=reads, writes=writes)

        def ts(eng, out, in0, s1, s2, op0, op1, reads, writes):
            if s2 is None:
                P.op(eng, lambda e: e.tensor_scalar(out, in0, s1, None, op0=op0), reads=reads, writes=writes)
            else:
                P.op(eng, lambda e: e.tensor_scalar(out, in0, s1, s2, op0=op0, op1=op1), reads=reads, writes=writes)

        def stt(eng, out, in0, scalar, in1, op0, op1, reads, writes):
            P.op(eng, lambda e: e.scalar_tensor_tensor(out=out, in0=in0, scalar=scalar, in1=in1, op0=op0, op1=op1),
                 reads=reads, writes=writes)

        def recip(out, in_, reads, writes):
            P.op("dve", lambda e: e.reciprocal(out, in_), reads=reads, writes=writes)

        def memset(eng, ap, val, writes):
            P.op(eng, lambda e: e.memset(ap, val), writes=writes)

        def dma(q, out, in_, reads, writes, key):
            return P.op(q, lambda e: e.dma_start(out=out, in_=in_), reads=reads, writes=writes, key=key)

        def wget(name, ti):
            nt, kc, ct, _ = cat[name]
            i = rr["w"] % NWS
            rr["w"] += 1
            n = kc * ct
            dma("sp", wsl[i][:, 0:n], scr[name][ti], [], [Twsl[i]], "w%d" % i)
            return wsl[i], Twsl[i]

        dma("pool", cstt[:], dr["cst"][:, :], [], [Tk], "cst")
        dma("pool", ident[:], dr["cmat"][:, 0:128], [], [Tk], "cst")
        dma("pool", trif[:], dr["cmat"][:, 128:256], [], [Tk], "cst")
        tcopy("dve", trib[:], trif[:], [Tk], [Tk])
        memset("dve", ones_bf[:], 1.0, [Tk])
        memset("dve", on2048[:], 1.0 / 2048.0, [Tk])
        memset("dve", on512[:], 1.0 / 512.0, [Tk])
        memset("pool", halo[:], 0.0, Thalo)

        def rms_rstd(src, onm):
            n = len(src)
            N = src[0][0].shape[-1]
            b = ps_alloc()
            for c, (ap, T) in enumerate(src):
                sq, Tsq = tb_alloc()
                act(sq[:, 0:N], ap, AF.Square, [T], [Tsq])
                mm(banks[b][:, 0:N], onm[:], sq[:, 0:N], c == 0, c == n - 1, [Tsq, Tk], [Tb[b]])
            sd, Tsd = tf_alloc()
            act(sd[:, 0:N], banks[b][:, 0:N], AF.Sqrt, [Tb[b], Tk], [Tsd], bias=cc(EPSC))
            recip(sd[:, 0:N], sd[:, 0:N], [Tsd], [Tsd])
            return sd[:, 0:N], Tsd

        def rms_apply(src, dst, g0, rstd, Trstd):
            for c, ((ap, T), (dap, dT)) in enumerate(zip(src, dst)):
                stt("dve", dap, ap, cc(g0 + c), rstd, ALU.mult, ALU.mult, [T, Trstd, Tk], [dT])

        def xchunks():
            return [(xT[:, c, :], TxT[c]) for c in range(16)]

        def hchunks():
            return [(hT[:, c, :], ThT[c]) for c in range(16)]

        def norm_x_to_h(g0):
            rstd, Tr = rms_rstd(xchunks(), on2048)
            rms_apply(xchunks(), hchunks(), g0, rstd, Tr)

        memT = catT[:, :, :].rearrange("p c t -> p (c t)").bitcast(F32).rearrange("p (c m) -> p c m", m=256)
        TmemT = Tcat
        memn = reg2[:, 0:4096].rearrange("p (c m) -> p c m", m=256)
        Tmemn = [Treg2[c // 2] for c in range(16)]
        mk_store = {}
        for mt in range(2):
            for cg in range(4):
                s_, Ts_ = tf_alloc()
                dma("sp", s_[:, :], dr["mem"][mt * 128:(mt + 1) * 128, cg * 512:(cg + 1) * 512], [], [Ts_], "stg%d" % rr["tfi"])
                b = ps_alloc()
                for c in range(4):
                    tr(banks[b][:, c * 128:(c + 1) * 128], s_[:, c * 128:(c + 1) * 128], [Ts_], [Tb[b]])
                evac(memT[:, cg * 4:(cg + 1) * 4, mt * 128:(mt + 1) * 128],
                     banks[b][:, :].rearrange("p (c t) -> p c t", t=128), [Tb[b]], TmemT[cg * 4:(cg + 1) * 4])
        for li, (ty, l, j) in enumerate(ltypes):
            msrc = [(memT[:, c, :], TmemT[c]) for c in range(16)]
            mdst = [(memn[:, c, :], Tmemn[c]) for c in range(16)]
            rstd, Tr = rms_rstd(msrc, on2048)
            rms_apply(msrc, mdst, GMEM + l * 16, rstd, Tr)
            for wi in range(4):
                wt, Tw = wget("mkv%d" % l, wi)
                wv = wt[:, 0:4096].rearrange("p (k c) -> p k c", c=256)
                if wi < 2:
                    for hh in range(2):
                        h = wi * 2 + hh
                        b = ps_alloc()
                        for kc in range(16):
                            mm(banks[b][:, 0:256], wv[:, kc, hh * 128:(hh + 1) * 128], memn[:, kc, :], kc == 0, kc == 15,
                               [Tw, Tmemn[kc]], [Tb[b]])
                        evac(mkb[:, h * 256:(h + 1) * 256], banks[b][:, 0:256], [Tb[b]], [Tmk])
                else:
                    for mt in range(2):
                        b = ps_alloc()
                        for kc in range(16):
                            mm(banks[b][:, 0:256], memn[:, kc, mt * 128:(mt + 1) * 128], wv[:, kc, :], kc == 0, kc == 15,
                               [Tw, Tmemn[kc]], [Tb[b]])
                        c0 = 1024 + mt * 512 + (wi - 2) * 256
                        evac(mkb[:, c0:c0 + 256], banks[b][:, 0:256], [Tb[b]], [Tmk])
            mk_store[li] = dma("sp", scr_mk[li], mkb[:, :], [Tmk], [], "mkst")

        def mem_attention(li):
            sc = 128.0 ** -0.5
            P.op("sp", lambda e: e.dma_start(out=mkb[:, :], in_=scr_mk[li]), writes=[Tmk], key="mkld", extra=[mk_store[li]])
            for h in range(4):
                pts = []
                for mc in range(2):
                    b = ps_alloc()
                    mm(banks[b][:, :], mkb[:, h * 256 + mc * 128:h * 256 + (mc + 1) * 128], reg2c[:, 8 + h, :], True, True,
                       [Tmk, Treg2[8 + h]], [Tb[b]])
                    pt, Tpt = tb_alloc()
                    act(pt[:, :], banks[b][:, :], AF.Exp, [Tb[b]], [Tpt], scale=sc)
                    pts.append((pt, Tpt))
                bo = ps_alloc()
                bd = ps_alloc()
                for mc in range(2):
                    pt, Tpt = pts[mc]
                    mm(banks[bo][:, :], mkb[:, 1024 + mc * 512 + h * 128:1024 + mc * 512 + (h + 1) * 128], pt[:, :], mc == 0, mc == 1,
                       [Tmk, Tpt], [Tb[bo]])
                for mc in range(2):
                    pt, Tpt = pts[mc]
                    mm(banks[bd][:, :], ones_bf[:], pt[:, :], mc == 0, mc == 1, [Tk, Tpt], [Tb[bd]])
                rd, Trd = tf_alloc()
                recip(rd[:, :], banks[bd][:, :], [Tb[bd]], [Trd])
                tt("dve", catT[:, 12 + h, :], banks[bo][:, :], rd[:, :], ALU.mult, [Tb[bo], Trd], [Tcat[12 + h]])

        def out_proj(l):
            for wi in range(8):
                wt, Tw = wget("out%d" % l, wi)
                wv = wt[:, 0:4096].rearrange("p (k c) -> p k c", c=256)
                for hh in range(2):
                    oc = wi * 2 + hh
                    b = ps_alloc()
                    for kc in range(16):
                        mm(banks[b][:, :], wv[:, kc, hh * 128:(hh + 1) * 128], catT[:, kc, :], kc == 0, kc == 15,
                           [Tw, Tcat[kc]], [Tb[b]])
                    tt("dve", xT[:, oc, :], xT[:, oc, :], banks[b][:, :], ALU.add, [Tb[b], TxT[oc]], [TxT[oc]])

        FGROUPS = [(0, 8), (8, 8), (16, 8), (24, 8), (32, 8), (40, 4)]

        def ffn_up(li, l, gi):
            f0, n = FGROUPS[gi]
            gb = (gi % 2) * 8
            for jj in range(n):
                f = f0 + jj
                wt, Tw = wget("up%d" % l, f)
                wv = wt[:, 0:4096].rearrange("p (k c) -> p k c", c=256)
                res = []
                for part in range(2):
                    fc = f + part * NFC
                    b = ps_alloc()
                    for kc in range(16):
                        mm(banks[b][:, :], wv[:, kc, part * 128:(part + 1) * 128], hT[:, kc, :], kc == 0, kc == 15,
                           [Tw, ThT[kc]], [Tb[b]])
                    ae, Ta = ae_alloc()
                    hoff = (li * 88 + fc) * 2
                    Th = Thalo[li * 88 + fc]
                    tcopy("pool", ae[:, 0:2], halo[:, hoff:hoff + 2], [Th], [Ta])
                    act(ae[:, 2:TB + 2], banks[b][:, :], AF.Copy, [Tb[b]], [Ta])
                    tcopy("pool", halo[:, hoff:hoff + 2], ae[:, TB:TB + 2], [Ta], [Th])
                    t1, T1 = tf_alloc()
                    w0 = cc(CW + (l * 3 + 0) * 88 + fc)
                    w1 = cc(CW + (l * 3 + 1) * 88 + fc)
                    w2 = cc(CW + (l * 3 + 2) * 88 + fc)
                    if part == 0:
                        ts("dve", t1[:, :], ae[:, 2:TB + 2], w2, cc(CB + l * 88 + fc), ALU.mult, ALU.add, [Ta, Tk], [T1])
                        stt("dve", t1[:, :], ae[:, 1:TB + 1], w1, t1[:, :], ALU.mult, ALU.add, [Ta, Tk, T1], [T1])
                        stt("dve", t1[:, :], ae[:, 0:TB], w0, t1[:, :], ALU.mult, ALU.add, [Ta, Tk, T1], [T1])
                    else:
                        t2, T2 = tf_alloc()
                        ts("pool", t1[:, :], ae[:, 2:TB + 2], w2, cc(CB + l * 88 + fc), ALU.mult, ALU.add, [Ta, Tk], [T1])
                        ts("pool", t2[:, :], ae[:, 1:TB + 1], w1, None, ALU.mult, None, [Ta, Tk], [T2])
                        tt("pool", t1[:, :], t1[:, :], t2[:, :], ALU.add, [T1, T2], [T1])
                        ts("pool", t2[:, :], ae[:, 0:TB], w0, None, ALU.mult, None, [Ta, Tk], [T2])
                        tt("pool", t1[:, :], t1[:, :], t2[:, :], ALU.add, [T1, T2], [T1])
                    res.append((t1, T1))
                (tg, Tg), (tv, Tv) = res
                act(tg[:, :], tg[:, :], AF.Silu, [Tg], [Tg])
                tt("dve", catT[:, gb + jj, :], tg[:, :], tv[:, :], ALU.mult, [Tg, Tv], [Tcat[gb + jj]])

        def ffn_down(li, l, gi):
            f0, n = FGROUPS[gi]
            gb = (gi % 2) * 8
            for q in range(4):
                if n == 8:
                    wt, Tw = wget("down%d" % l, gi * 4 + q)
                else:
                    wt, Tw = wget("downt%d" % l, q)
                wv = wt[:, 0:n * 512].rearrange("p (k c) -> p k c", c=512)
                for o4 in range(4):
                    oc = q * 4 + o4
                    b = ps_alloc()
                    for kc in range(n):
                        mm(banks[b][:, :], wv[:, kc, o4 * 128:(o4 + 1) * 128], catT[:, gb + kc, :], kc == 0, kc == n - 1,
                           [Tw, Tcat[gb + kc]], [Tb[b]])
                    tt("dve", xT[:, oc, :], xT[:, oc, :], banks[b][:, :], ALU.add, [Tb[b], TxT[oc]], [TxT[oc]])

        def ffn(li, l):
            norm_x_to_h(GFFN + l * 16)
            ffn_up(li, l, 0)
            for gi in range(len(FGROUPS)):
                if gi + 1 < len(FGROUPS):
                    ffn_up(li, l, gi + 1)
                ffn_down(li, l, gi)

        def mixer_A(li, l, ai):
            norm_x_to_h(GMIX + l * 16)
            for wi in range(6):
                wt, Tw = wget("ina%d" % l, wi)
                wv = wt[:, 0:4096].rearrange("p (k c) -> p k c", c=256)
                for t4 in range(4):
                    b = ps_alloc()
                    for kc in range(16):
                        mm(banks[b][:, 0:256], hT[:, kc, t4 * 128:(t4 + 1) * 128], wv[:, kc, :], kc == 0, kc == 15,
                           [Tw, ThT[kc]], [Tb[b]])
                    act(reg2v[:, t4, wi * 256:(wi + 1) * 256], banks[b][:, 0:256], AF.Gelu, [Tb[b]], Treg2[t4 * 3:t4 * 3 + 3])
            for t4 in range(4):
                Tv3 = Treg2[t4 * 3:t4 * 3 + 3]
                ss = smalls[:, t4:t4 + 1]
                memset("pool", ss, 0.0, [Tsm[t4]])
                act(catT[:, 12:15, :].rearrange("p c t -> p (c t)"), reg2v[:, t4, :], AF.Square, Tv3, Tcat[12:15] + [Tsm[t4]], accum_out=ss)
                act(ss, ss, AF.Sqrt, [Tsm[t4], Tk], [Tsm[t4]], bias=cc(EPSC), scale=1.0 / 1536.0)
                recip(ss, ss, [Tsm[t4]], [Tsm[t4]])
                ts("dve", reg2v[:, t4, :], reg2v[:, t4, :], ss, None, ALU.mult, None, Tv3 + [Tsm[t4]], Tv3)
            dma("sp", wspb[:, :], scr["sp%d" % l][0], [], [Twsp], "wsp")
            dma("sp", bspt[:, :], dr["bsp"][:, ai * 1536:(ai + 1) * 1536], [], [Tbsp], "bsp")
            wsp = wspb
            for wi in range(6):
                wt, Tw = wget("ina%d" % l, 6 + wi)
                wv = wt[:, 0:4096].rearrange("p (k c) -> p k c", c=256)
                for hh in range(2):
                    g = wi * 2 + hh
                    bu = ps_alloc()
                    for kc in range(16):
                        mm(banks[bu][:, :], wv[:, kc, hh * 128:(hh + 1) * 128], hT[:, kc, :], kc == 0, kc == 15,
                           [Tw, ThT[kc]], [Tb[bu]])
                    ug, Tug = tf_alloc()
                    act(ug[:, :], banks[bu][:, :], AF.Gelu, [Tb[bu]], [Tug])
                    bs = ps_alloc()
                    for t4 in range(4):
                        mm(banks[bs][:, t4 * 128:(t4 + 1) * 128], reg2v[:, t4, g * 128:(g + 1) * 128], wsp[:, g * 128:(g + 1) * 128],
                           True, True, Treg2[t4 * 3:t4 * 3 + 3] + [Twsp], [Tb[bs]])
                    sv, Tsv = tf_alloc()
                    for t4 in range(4):
                        stt("dve", sv[:, t4 * 128:(t4 + 1) * 128], banks[bs][:, t4 * 128:(t4 + 1) * 128], cc(GV + l * 12 + g),
                            bspt[:, g * 128:(g + 1) * 128], ALU.mult, ALU.add, [Tb[bs], Tk, Tbsp], [Tsv])
                    tt("dve", catT[:, g, :], sv[:, :], ug[:, :], ALU.mult, [Tsv, Tug], [Tcat[g]])
            for wi in range(2):
                wt, Tw = wget("ina%d" % l, 12 + wi)
                wv = wt[:, 0:4096].rearrange("p (k c) -> p k c", c=256)
                for hh in range(2):
                    h = wi * 2 + hh
                    b = ps_alloc()
                    for kc in range(16):
                        mm(banks[b][:, :], wv[:, kc, hh * 128:(hh + 1) * 128], hT[:, kc, :], kc == 0, kc == 15,
                           [Tw, ThT[kc]], [Tb[b]])
                    evac(reg2c[:, 8 + h, :], banks[b][:, :], [Tb[b]], [Treg2[8 + h]])
            mem_attention(li)
            out_proj(l)

        spbuf = {}

        def wsl_sp(l):
            if l not in spbuf:
                t_ = sb("wsp%d" % l, [128, 1536], BF16)
                T_ = Tl()
                dma("sp", t_[:, :], scr["sp%d" % l][0], [], [T_], "wsp%d" % l)
                spbuf[l] = (t_, T_)
            return spbuf[l]

        def rope_tables(t):
            dma("sp", posi[:, :], dr["posb"][:, t * TB:(t + 1) * TB], [], [Tposi], "posi")
            for which, (dst, Td) in enumerate(((sin2, Tsin), (cos2, Tcos))):
                a, Ta = tf_alloc()
                u, Tu = tf_alloc()
                av, uv = a[0:64, :], u[0:64, :]
                tcopy("dve", av, posi[:, :], [Tposi], [Ta])
                ts("dve", av, av, cstt[0:64, INV:INV + 1], None, ALU.mult, None, [Ta, Tk], [Ta])
                if which == 1:
                    ts("dve", av, av, float(np.pi / 2), None, ALU.add, None, [Ta], [Ta])
                kt_, Tkt = tf_alloc()
                kiv = kt_[0:64, :].bitcast(I32)
                ts("dve", kiv, av, float(1.0 / (2 * np.pi)), None, ALU.mult, None, [Ta], [Tkt])
                tcopy("dve", uv, kiv, [Tkt], [Tu])
                stt("dve", av, uv, float(-C1), av, ALU.mult, ALU.add, [Tu, Ta], [Ta])
                stt("dve", av, uv, float(-C2), av, ALU.mult, ALU.add, [Tu, Ta], [Ta])
                ts("dve", av, av, float(PI_SAFE), float(-PI_SAFE), ALU.min, ALU.max, [Ta], [Ta])
                if which == 0:
                    act(dst[:, :], av, AF.Sin, [Ta, Tk], [Td], scale=cstt[0:64, SGN:SGN + 1])
                else:
                    act(dst[:, :], av, AF.Sin, [Ta], [Td])

        def rope_apply(ba, bb, dst, reads, writes):
            k1, T1 = tf_alloc()
            k2, T2 = tf_alloc()
            tt("dve", k1[0:64, :], ba, cos2[:, :], ALU.mult, reads + [Tcos], [T1])
            tt("dve", k2[0:64, :], bb, sin2[:, :], ALU.mult, reads + [Tsin], [T2])
            tt("dve", dst, k1[0:64, :], k2[0:64, :], ALU.add, [T1, T2], writes)

        def kv_build(t):
            norm_x_to_h(GKV)
            kvf = []
            for wi in range(4):
                wt, Tw = wget("kva", wi)
                wv = wt[:, 0:2048].rearrange("p (k c) -> p k c", c=128)
                b = ps_alloc()
                for kc in range(16):
                    mm(banks[b][:, :], wv[:, kc, :], hT[:, kc, :], kc == 0, kc == 15, [Tw, ThT[kc]], [Tb[b]])
                f_, Tf_ = tf_alloc()
                evac(f_[:, :], banks[b][:, :], [Tb[b]], [Tf_])
                kvf.append((f_[:, :], Tf_))
            wt, Tw = wget("kva", 4)
            wv = wt[:, 0:2048].rearrange("p (k c) -> p k c", c=128)
            ba, bb = ps_alloc(), ps_alloc()
            for kc in range(16):
                mm(banks[ba][0:64, :], wv[:, kc, 0:64], hT[:, kc, :], kc == 0, kc == 15, [Tw, ThT[kc]], [Tb[ba]])
            for kc in range(16):
                mm(banks[bb][0:64, :], wv[:, kc, 64:128], hT[:, kc, :], kc == 0, kc == 15, [Tw, ThT[kc]], [Tb[bb]])
            rope_apply(banks[ba][0:64, :], banks[bb][0:64, :], kropeT[:, t * TB:(t + 1) * TB], [Tb[ba], Tb[bb]], [Tkrope[t]])
            rstd, Tr = rms_rstd(kvf, on512)
            dst = [(ckvT[:, rc, t * TB:(t + 1) * TB], TckvT[rc][t]) for rc in range(4)]
            rms_apply(kvf, dst, GKVL, rstd, Tr)

        def mixer_B(li, l, j, t):
            norm_x_to_h(GMIX + l * 16)
            qlf = []
            for wi in range(4):
                wt, Tw = wget("inb%d" % j, wi)
                wv = wt[:, 0:4096].rearrange("p (k c) -> p k c", c=256)
                for hh in range(2):
                    c = wi * 2 + hh
                    b = ps_alloc()
                    for kc in range(16):
                        mm(banks[b][:, :], wv[:, kc, hh * 128:(hh + 1) * 128], hT[:, kc, :], kc == 0, kc == 15,
                           [Tw, ThT[kc]], [Tb[b]])
                    if c < 4:
                        f_, Tf_ = tf_alloc()
                        evac(f_[:, :], banks[b][:, :], [Tb[b]], [Tf_])
                        qlf.append((f_[:, :], Tf_))
                    else:
                        evac(reg2c[:, 8 + (c - 4), :], banks[b][:, :], [Tb[b]], [Treg2[8 + c - 4]])
            rstd, Tr = rms_rstd(qlf, on512)
            qln = [(qlnb[:, rc, :], Tqln[rc]) for rc in range(4)]
            rms_apply(qlf, qln, GQL + j * 4, rstd, Tr)
            mem_attention(li)
            sc = 192.0 ** -0.5
            for th in range(3):
                wt, Tw = wget("uqn%d" % j, th)
                wv = wt[:, 0:2048].rearrange("p (k c) -> p k c", c=512)
                for hh in range(4):
                    b = ps_alloc()
                    for kc in range(4):
                        mm(banks[b][:, :], wv[:, kc, hh * 128:(hh + 1) * 128], qln[kc][0], kc == 0, kc == 3,
                           [Tw, qln[kc][1]], [Tb[b]])
                    evac(reg2c[:, hh, :], banks[b][:, :], [Tb[b]], [Treg2[hh]])
                wt, Tw = wget("uqr%d" % j, th)
                wv = wt[:, 0:2048].rearrange("p (k c) -> p k c", c=512)
                for hh in range(4):
                    ba, bb = ps_alloc(), ps_alloc()
                    for kc in range(4):
                        mm(banks[ba][0:64, :], wv[:, kc, hh * 128:hh * 128 + 64], qln[kc][0], kc == 0, kc == 3,
                           [Tw, qln[kc][1]], [Tb[ba]])
                    for kc in range(4):
                        mm(banks[bb][0:64, :], wv[:, kc, hh * 128 + 64:hh * 128 + 128], qln[kc][0], kc == 0, kc == 3,
                           [Tw, qln[kc][1]], [Tb[bb]])
                    rope_apply(banks[ba][0:64, :], banks[bb][0:64, :], reg2c[0:64, 4 + hh, :], [Tb[ba], Tb[bb]], [Treg2[4 + hh]])
                wk, Twk = wget("uk%d" % j, th)
                wkv = wk[:, 0:2048].rearrange("p (k c) -> p k c", c=512)
                wu, Twu = wget("uv%d" % j, th)
                wuv = wu[:, 0:2048].rearrange("p (k c) -> p k c", c=512)
                for hh in range(4):
                    h = th * 4 + hh
                    bacc, bden = ps_alloc(), ps_alloc()
                    pinned.add(bacc)
                    pinned.add(bden)
                    first = True
                    for kg in range(t + 1):
                        b = ps_alloc()
                        for rc in range(4):
                            mm(banks[b][:, :], wkv[:, rc, hh * 128:(hh + 1) * 128], ckvT[:, rc, kg * TB:(kg + 1) * TB], rc == 0, rc == 3,
                               [Twk, TckvT[rc][kg]], [Tb[b]])
                        kh, Tkh = tb_alloc()
                        evac(kh[:, :], banks[b][:, :], [Tb[b]], [Tkh])
                        b = ps_alloc()
                        for jx in range(4):
                            for rc in range(4):
                                mm(banks[b][:, jx * 128:(jx + 1) * 128], ckvT[:, rc, kg * TB + jx * 128:kg * TB + (jx + 1) * 128],
                                   wuv[:, rc, hh * 128:(hh + 1) * 128], rc == 0, rc == 3, [Twu, TckvT[rc][kg]], [Tb[b]])
                        vh, Tvh = tb_alloc()
                        evac(vh[:, :], banks[b][:, :], [Tb[b]], [Tvh])
                        for jx in range(4):
                            kt = kg * 4 + jx
                            c0 = jx * 128 if kg == t else 0
                            b = ps_alloc()
                            mm(banks[b][:, c0:TB], kh[:, jx * 128:(jx + 1) * 128], reg2c[:, hh, c0:TB], True, False,
                               [Tkh, Treg2[hh]], [Tb[b]])
                            mm(banks[b][:, c0:TB], kropeT[:, kt * 128:(kt + 1) * 128], reg2c[0:64, 4 + hh, c0:TB], False, True,
                               [Tkrope[kg], Treg2[4 + hh]], [Tb[b]])
                            pt, Tpt = tb_alloc()
                            act(pt[:, c0:TB], banks[b][:, c0:TB], AF.Exp, [Tb[b]], [Tpt], scale=sc)
                            if kg == t:
                                tt("pool", pt[:, c0:c0 + 128], pt[:, c0:c0 + 128], trib[:], ALU.mult, [Tpt, Tk], [Tpt])
                            mm(banks[bacc][:, c0:TB], vh[:, jx * 128:(jx + 1) * 128], pt[:, c0:TB], first, False, [Tvh, Tpt], [Tb[bacc]])
                            mm(banks[bden][:, c0:TB], ones_bf[:], pt[:, c0:TB], first, False, [Tk, Tpt], [Tb[bden]])
                            first = False
                    rd, Trd = tf_alloc()
                    recip(rd[:, :], banks[bden][:, :], [Tb[bden]], [Trd])
                    tt("dve", catT[:, h, :], banks[bacc][:, :], rd[:, :], ALU.mult, [Tb[bacc], Trd], [Tcat[h]])
                    pinned.discard(bacc)
                    pinned.discard(bden)
            out_proj(l)

        final_stores = []
        for t in range(NB):
            for t4 in range(4):
                for cg in range(4):
                    s_, Ts_ = tf_alloc()
                    r0 = t * TB + t4 * 128
                    dma("sp", s_[:, :], dr["x"][r0:r0 + 128, cg * 512:(cg + 1) * 512], [], [Ts_], "stg%d" % rr["tfi"])
                    b = ps_alloc()
                    for c in range(4):
                        tr(banks[b][:, c * 128:(c + 1) * 128], s_[:, c * 128:(c + 1) * 128], [Ts_], [Tb[b]])
                    evac(xT[:, cg * 4:(cg + 1) * 4, t4 * 128:(t4 + 1) * 128],
                         banks[b][:, :].rearrange("p (c t) -> p c t", t=128), [Tb[b]], TxT[cg * 4:(cg + 1) * 4])
            if hasB:
                rope_tables(t)
            ai = 0
            kv_done = False
            for li, (ty, l, j) in enumerate(ltypes):
                if ty == "A":
                    mixer_A(li, l, ai)
                    ai += 1
                else:
                    if not kv_done:
                        kv_build(t)
                        kv_done = True
                    mixer_B(li, l, j, t)
                ffn(li, l)
            rstd0, Tr0 = rms_rstd(xchunks(), on2048)
            tcopy("dve", rstd_fin[:, :], rstd0, [Tr0], [Trfin])
            rstd, Tr = rstd_fin[:, :], Trfin
            for cg in range(4):
                ys = []
                for c4 in range(4):
                    c = cg * 4 + c4
                    y_, Ty_ = tf_alloc()
                    stt("dve", y_[:, :], xT[:, c, :], cc(GFIN + c), rstd, ALU.mult, ALU.mult, [TxT[c], Tr, Tk], [Ty_])
                    ys.append((y_, Ty_))
                for t4 in range(4):
                    b = ps_alloc()
                    for c4 in range(4):
                        tr(banks[b][:, c4 * 128:(c4 + 1) * 128], ys[c4][0][:, t4 * 128:(t4 + 1) * 128], [ys[c4][1]], [Tb[b]])
                    o_, To_ = tf_alloc()
                    evac(o_[:, :], banks[b][:, :], [Tb[b]], [To_])
                    r0 = t * TB + t4 * 128
                    final_stores.append(dma("pool", y[r0:r0 + 128, cg * 512:(cg + 1) * 512], o_[:, :], [To_], [],
                                            "ost%d" % rr["tfi"]))
        lastk = {}
        for o in final_stores:
            lastk[o.key] = o
        P.wait("pool", list(lastk.values()))
        P.emit(block)
    return nc


LT_FULL = [("A", 0, 0), ("A", 1, 1), ("B", 2, 0), ("B", 3, 1)]


def host_inputs(inp, ltypes, S, b):
    f = lambda a: np.ascontiguousarray(np.asarray(a, dtype=np.float32))
    m = {}
    m["x"] = f(inp["x"][b, :S])
    m["mem"] = f(inp["mem"][b])
    m["posb"] = np.ascontiguousarray(np.broadcast_to(np.asarray(inp["positions"][b, :S], dtype=np.int32)[None, :], (64, S)))
    cst = np.zeros((128, NC_), np.float32)

    def pm(v):
        return np.asarray(v, np.float32).reshape(-1, 128).T

    for l in range(4):
        cst[:, GMIX + l * 16:GMIX + (l + 1) * 16] = pm(inp["g_mix"][l])
        cst[:, GFFN + l * 16:GFFN + (l + 1) * 16] = pm(inp["g_ffn"][l])
        cst[:, GMEM + l * 16:GMEM + (l + 1) * 16] = pm(inp["g_mem"][l])
        for k in range(3):
            cst[:, CW + (l * 3 + k) * 88:CW + (l * 3 + k + 1) * 88] = pm(inp["conv_w"][l, k])
        cst[:, CB + l * 88:CB + (l + 1) * 88] = pm(inp["conv_b"][l])
    cst[:, GFIN:GFIN + 16] = pm(inp["g_final"])
    cst[:, GKV:GKV + 16] = pm(inp["g_kv"])
    cst[:, GKVL:GKVL + 4] = pm(inp["g_kv_lat"])
    for j in range(2):
        cst[:, GQL + j * 4:GQL + (j + 1) * 4] = pm(inp["g_q_lat"][j])
        cst[:, GV + j * 12:GV + (j + 1) * 12] = pm(inp["g_v"][j])
    inv = (1.0 / (10000.0 ** (np.arange(0, 64, 2, dtype=np.float32) / np.float32(64)))).astype(np.float32)
    cst[0:32, INV] = inv
    cst[32:64, INV] = inv
    cst[0:32, SGN] = -1.0
    cst[32:64, SGN] = 1.0
    cst[:, EPSC] = EPS
    m["cst"] = cst
    cm = np.zeros((128, 256), np.float32)
    cm[:, 0:128] = np.eye(128, dtype=np.float32)
    cm[:, 128:256] = np.triu(np.ones((128, 128), np.float32))
    m["cmat"] = cm
    nA = sum(1 for t in ltypes if t[0] == "A")
    bsp = np.zeros((128, max(nA, 1) * 1536), np.float32)
    ai = 0
    for (ty, l, j) in ltypes:
        if ty == "A":
            bsp[:, ai * 1536:(ai + 1) * 1536] = np.asarray(inp["b_sp"][l], np.float32).reshape(1, 1536)
            ai += 1
            m["w_in_a%d" % l] = f(inp["w_in_a"][l])
            m["w_spt%d" % l] = f(np.transpose(np.asarray(inp["w_sp"][l]), (2, 0, 1)).reshape(128, 1536))
        else:
            m["w_in_b%d" % j] = f(inp["w_in_b"][j])
            wq = np.asarray(inp["w_uq"][j], np.float32).reshape(512, 12, 192)
            m["w_uqn%d" % j] = f(wq[:, :, 0:128].reshape(512, 1536))
            m["w_uqr%d" % j] = f(np.concatenate([wq[:, :, 128:192], wq[:, :, 160:192], wq[:, :, 128:160]], axis=2).reshape(512, 1536))
            m["w_uk%d" % j] = f(np.asarray(inp["w_uk"][j]).reshape(512, 1536))
            m["w_uv%d" % j] = f(np.asarray(inp["w_uv"][j]).reshape(512, 1536))
        m["w_out%d" % l] = f(inp["w_out"][l])
        m["w_up%d" % l] = f(inp["w_ffn_up"][l])
        m["w_down%d" % l] = f(inp["w_ffn_down"][l])
        m["w_mkv%d" % l] = f(inp["w_mem_kv"][l])
    m["bsp"] = bsp
    if any(t[0] == "B" for t in ltypes):
        wk = np.asarray(inp["w_kv_a"], np.float32)
        m["w_kva"] = f(np.concatenate([wk[:, 0:576], wk[:, 544:576], wk[:, 512:544]], axis=1))
    return m


_CACHE = {}


def run(inp, ltypes, S, ncores=8):
    key = (S, tuple(ltypes))
    if key not in _CACHE:
        _CACHE[key] = build(S, ltypes)
    nc = _CACHE[key]
    shared = None
    in_maps = []
    for b in range(ncores):
        mb = host_inputs(inp, ltypes, S, b) if shared is None else dict(shared)
        if shared is None:
            shared = mb
        else:
            f = lambda a: np.ascontiguousarray(np.asarray(a, dtype=np.float32))
            mb["x"] = f(inp["x"][b, :S])
            mb["mem"] = f(inp["mem"][b])
            mb["posb"] = np.ascontiguousarray(np.broadcast_to(np.asarray(inp["positions"][b, :S], dtype=np.int32)[None, :], (64, S)))
        in_maps.append(mb)
    res = run_bass_kernel_spmd(nc, in_maps, core_ids=list(range(ncores)))
    return np.stack([np.asarray(r["y"], dtype=np.float32) for r in res.results], axis=0)


def kernel(**inputs):
    return run(inputs, LT_FULL, 4096, 8)
```

```python
import contextlib
import numpy as np
import concourse.bass as bass
import concourse.mybir as mybir
from concourse.bass_utils import run_bass_kernel_spmd

F32 = mybir.dt.float32
BF16 = mybir.dt.bfloat16
I32 = mybir.dt.int32
ALU = mybir.AluOpType
AF = mybir.ActivationFunctionType

D = 2048
DFF = 5632
NFC = 44
TB = 512
EPS = 1e-6
ENGS = ("pe", "act", "dve", "pool", "sp")
EPOCH = 30000

GMIX, GFFN, GMEM, GFIN, GKV, GKVL, GQL, GV = 0, 64, 128, 192, 208, 224, 228, 236
CW = 260
CB = CW + 4 * 3 * 88
INV = CB + 4 * 88
SGN = INV + 1
EPSC = INV + 2
NC_ = INV + 4
PI_SAFE = 3.1415925
C1 = 6.28125
C2 = 2.0 * np.pi - 6.28125


class Tl:
    __slots__ = ("w", "r")

    def __init__(self):
        self.w = {}
        self.r = {}


class Op:
    __slots__ = ("eng", "fn", "deps", "needed", "sem", "val", "key", "stream")


class Prog:
    def __init__(self, nc, stack):
        self.nc = nc
        self.stack = stack
        self.ops = {e: [] for e in ENGS}

    def op(self, eng, fn, reads=(), writes=(), key=None, acc=False, extra=()):
        o = Op()
        o.eng = eng
        o.fn = fn
        o.key = key
        o.needed = False
        o.sem = None
        o.val = 0
        st = o.stream = key if key is not None else eng
        deps = set(extra)
        for t in reads:
            for s, p in t.w.items():
                if s == st and key is not None:
                    continue
                deps.add(p)
        for t in writes:
            for s, p in t.w.items():
                if s == st and (key is not None or acc):
                    continue
                deps.add(p)
            for s, p in t.r.items():
                if s == st and key is not None:
                    continue
                deps.add(p)
        for p in deps:
            p.needed = True
        o.deps = deps
        for t in reads:
            t.r[st] = o
        for t in writes:
            t.w = {st: o}
            t.r = {}
        self.ops[eng].append(o)
        return o

    def wait(self, eng, ops):
        return self.op(eng, None, extra=[o for o in ops if o is not None])

    def emit(self, block):
        nc, stack = self.nc, self.stack
        keysem, keycnt = {}, {}
        for e in ENGS:
            cnt, sem, nsem = 0, None, 0
            for o in self.ops[e]:
                if o.key is not None:
                    if o.key not in keysem:
                        keysem[o.key] = stack.enter_context(nc.semaphore("k_" + str(o.key)))
                        keycnt[o.key] = 0
                    keycnt[o.key] += 1
                    o.sem = keysem[o.key]
                    o.val = 16 * keycnt[o.key]
                elif o.needed and o.fn is not None:
                    if sem is None or cnt >= EPOCH:
                        sem = stack.enter_context(nc.semaphore("e_%s_%d" % (e, nsem)))
                        nsem += 1
                        cnt = 0
                    cnt += 1
                    o.sem = sem
                    o.val = cnt

        def run(e, eng):
            seen = {}
            for o in self.ops[e]:
                waits = {}
                for p in o.deps:
                    sid = id(p.sem)
                    if seen.get(sid, 0) >= p.val:
                        continue
                    if sid not in waits or waits[sid][1] < p.val:
                        waits[sid] = (p.sem, p.val)
                wl = list(waits.values())
                for s, v in wl:
                    seen[id(s)] = v
                if o.fn is None:
                    for s, v in wl:
                        eng.wait_ge(s, v)
                    continue
                for s, v in wl[1:]:
                    eng.wait_ge(s, v)
                ins = o.fn(eng)
                if wl:
                    ins._wait_ge(wl[0][0], wl[0][1])
                if o.key is not None:
                    ins.then_inc(o.sem, 16)
                elif o.sem is not None:
                    ins.then_inc(o.sem, 1)

        block.tensor(lambda eng: run("pe", eng))
        block.scalar(lambda eng: run("act", eng))
        block.vector(lambda eng: run("dve", eng))
        block.gpsimd(lambda eng: run("pool", eng))
        block.sync(lambda eng: run("sp", eng))


def wcat(ltypes):
    cat = {}
    for (ty, l, j) in ltypes:
        if ty == "A":
            def ina(i, l=l):
                if i < 6:
                    c0 = 1536 + 256 * i
                elif i < 12:
                    c0 = 256 * (i - 6)
                else:
                    c0 = 3072 + 256 * (i - 12)
                return [("w_in_a%d" % l, 0, c0, 256, 0)]
            cat["ina%d" % l] = (14, 16, 256, ina)
        else:
            cat["inb%d" % j] = (4, 16, 256, lambda i, j=j: [("w_in_b%d" % j, 0, 256 * i, 256, 0)])
            for nm in ("uqn", "uqr", "uk", "uv"):
                cat["%s%d" % (nm, j)] = (3, 4, 512, lambda i, j=j, nm=nm: [("w_%s%d" % (nm, j), 0, 512 * i, 512, 0)])
        cat["out%d" % l] = (8, 16, 256, lambda i, l=l: [("w_out%d" % l, 0, 256 * i, 256, 0)])
        cat["up%d" % l] = (NFC, 16, 256, lambda i, l=l: [("w_up%d" % l, 0, 128 * i, 128, 0),
                                                         ("w_up%d" % l, 0, DFF + 128 * i, 128, 128)])
        cat["down%d" % l] = (20, 8, 512, lambda i, l=l: [("w_down%d" % l, 8 * (i // 4), 512 * (i % 4), 512, 0)])
        cat["downt%d" % l] = (4, 4, 512, lambda i, l=l: [("w_down%d" % l, 40, 512 * i, 512, 0)])
        cat["mkv%d" % l] = (4, 16, 256, lambda i, l=l: [("w_mkv%d" % l, 0, 256 * i, 256, 0)])
    if any(t[0] == "B" for t in ltypes):
        cat["kva"] = (5, 16, 128, lambda i: [("w_kva", 0, 128 * i, 128, 0)])
    return cat


def wshapes(ltypes):
    sh = {}
    for (ty, l, j) in ltypes:
        if ty == "A":
            sh["w_in_a%d" % l] = (D, 3584)
            sh["w_spt%d" % l] = (128, 1536)
        else:
            sh["w_in_b%d" % j] = (D, 1024)
            for nm in ("uqn", "uqr", "uk", "uv"):
                sh["w_%s%d" % (nm, j)] = (512, 1536)
        sh["w_out%d" % l] = (D, D)
        sh["w_up%d" % l] = (D, 2 * DFF)
        sh["w_down%d" % l] = (DFF, D)
        sh["w_mkv%d" % l] = (D, 1024)
    if any(t[0] == "B" for t in ltypes):
        sh["w_kva"] = (D, 640)
    return sh


def build(S, ltypes):
    NB = S // TB
    nA = sum(1 for t in ltypes if t[0] == "A")
    hasB = any(t[0] == "B" for t in ltypes)
    nc = bass.Bass("TRN2", target_bir_lowering=False)
    dr = {}
    dr["x"] = nc.dram_tensor("x", [S, D], F32, kind="ExternalInput").ap()
    dr["mem"] = nc.dram_tensor("mem", [256, D], F32, kind="ExternalInput").ap()
    dr["posb"] = nc.dram_tensor("posb", [64, S], I32, kind="ExternalInput").ap()
    dr["cst"] = nc.dram_tensor("cst", [128, NC_], F32, kind="ExternalInput").ap()
    dr["cmat"] = nc.dram_tensor("cmat", [128, 256], F32, kind="ExternalInput").ap()
    dr["bsp"] = nc.dram_tensor("bsp", [128, max(nA, 1) * 1536], F32, kind="ExternalInput").ap()
    for nm, shp in wshapes(ltypes).items():
        dr[nm] = nc.dram_tensor(nm, list(shp), F32, kind="ExternalInput").ap()
    y = nc.dram_tensor("y", [S, D], F32, kind="ExternalOutput").ap()
    cat = wcat(ltypes)
    scr = {}
    for nm, (nt, kc, ct, _) in cat.items():
        scr[nm] = nc.dram_tensor("scr_" + nm, [nt, 128, kc * ct], BF16).ap()
    for (ty, l, j) in ltypes:
        if ty == "A":
            scr["sp%d" % l] = nc.dram_tensor("scr_sp%d" % l, [1, 128, 1536], BF16).ap()
    scr_mk = nc.dram_tensor("scr_mk", [max(1, len(ltypes)), 128, 2048], BF16).ap()

    with contextlib.ExitStack() as st:
        P = Prog(nc, st)
        NS = 6
        stg = [st.enter_context(nc.sbuf_tensor("stg%d" % i, [128, 4096], F32)) for i in range(NS)]
        cvt = [st.enter_context(nc.sbuf_tensor("cvt%d" % i, [128, 4096], BF16)) for i in range(NS)]
        trif = st.enter_context(nc.sbuf_tensor("trif", [128, 128], F32))
        Ttri = Tl()
        Ts = [Tl() for _ in range(NS)]
        Tc = [Tl() for _ in range(NS)]
        block = st.enter_context(nc.Block())
        P.op("sp", lambda e: e.dma_start(out=trif[:], in_=dr["cmat"][:, 128:256]), writes=[Ttri], key="tri")
        cnt = [0]
        stores = []

        def convert(dst_ap, n, pieces, mask=False):
            i = cnt[0] % NS
            cnt[0] += 1
            for (dst_v, src_v) in pieces(stg[i]):
                P.op("sp", (lambda d, s: lambda e: e.dma_start(out=d, in_=s))(dst_v, src_v), writes=[Ts[i]], key="ps%d" % i)
            if mask:
                a = stg[i][:, 0:n].rearrange("p (g t) -> p g t", t=128)
                o = cvt[i][:, 0:n].rearrange("p (g t) -> p g t", t=128)
                for g in range(12):
                    P.op("dve", (lambda o, a: lambda e: e.tensor_tensor(out=o, in0=a, in1=trif[:], op=ALU.mult))(o[:, g, :], a[:, g, :]),
                         reads=[Ts[i], Ttri], writes=[Tc[i]])
            elif cnt[0] % 3 == 0:
                P.op("act", (lambda i: lambda e: e.activation(cvt[i][:, 0:n], stg[i][:, 0:n], AF.Copy))(i), reads=[Ts[i]], writes=[Tc[i]])
            else:
                P.op("dve", (lambda i: lambda e: e.tensor_copy(cvt[i][:, 0:n], stg[i][:, 0:n]))(i), reads=[Ts[i]], writes=[Tc[i]])
            stores.append(P.op("pool", (lambda i: lambda e: e.dma_start(out=dst_ap, in_=cvt[i][:, 0:n]))(i), reads=[Tc[i]], key="pc%d" % i))

        for nm, (nt, kc, ct, pf) in cat.items():
            for ti in range(nt):
                def pieces(stg_t, ti=ti, kc=kc, ct=ct, pf=pf):
                    res = []
                    sv = stg_t[:, 0:kc * ct].rearrange("p (k c) -> p k c", c=ct)
                    for (src, r0, c0, ncol, d0) in pf(ti):
                        srcv = dr[src].rearrange("(k p) n -> p k n", p=128)[:, r0:r0 + kc, c0:c0 + ncol]
                        res.append((sv[:, :, d0:d0 + ncol], srcv))
                    return res
                convert(scr[nm][ti], kc * ct, pieces)
        for (ty, l, j) in ltypes:
            if ty == "A":
                convert(scr["sp%d" % l][0], 1536, (lambda l: lambda stg_t: [(stg_t[:, 0:1536], dr["w_spt%d" % l][:, :])])(l), mask=True)
        P.wait("pool", stores[-2 * NS:])
        P.emit(block)

    with contextlib.ExitStack() as st:
        P = Prog(nc, st)

        def sb(name, shape, dt):
            return st.enter_context(nc.sbuf_tensor(name, shape, dt))

        xT = sb("xT", [128, 16, TB], F32)
        TxT = [Tl() for _ in range(16)]
        hT = sb("hT", [128, 16, TB], BF16)
        ThT = [Tl() for _ in range(16)]
        catT = sb("catT", [128, 16, TB], BF16)
        Tcat = [Tl() for _ in range(16)]
        reg2 = sb("reg2", [128, 12 * TB], BF16)
        Treg2 = [Tl() for _ in range(12)]
        reg2c = reg2[:, :].rearrange("p (c t) -> p c t", t=TB)
        reg2v = reg2[:, :].rearrange("p (a v) -> p a v", v=1536)
        if hasB:
            ckvT = sb("ckvT", [128, 4, S], BF16)
            TckvT = [[Tl() for _ in range(NB)] for _ in range(4)]
            kropeT = sb("kropeT", [64, S], BF16)
            Tkrope = [Tl() for _ in range(NB)]
        nL = len(ltypes)
        mkb = sb("mkb", [128, 2048], BF16)
        Tmk = Tl()
        qlnb = sb("qlnb", [128, 4, TB], BF16)
        Tqln = [Tl() for _ in range(4)]
        rstd_fin = sb("rstd_fin", [128, TB], F32)
        Trfin = Tl()
        wspb = sb("wspb", [128, 1536], BF16)
        Twsp = Tl()
        NWS = 3
        wsl = [sb("wsl%d" % i, [128, 4096], BF16) for i in range(NWS)]
        Twsl = [Tl() for _ in range(NWS)]
        cstt = sb("cstt", [128, NC_], F32)
        ident = sb("ident", [128, 128], F32)
        trif = sb("trif2", [128, 128], F32)
        trib = sb("trib", [128, 128], BF16)
        ones_bf = sb("ones_bf", [128, 128], BF16)
        on2048 = sb("on2048", [128, 128], BF16)
        on512 = sb("on512", [128, 128], BF16)
        bspt = sb("bspt", [128, 1536], F32)
        Tbsp = Tl()
        halo = sb("halo", [128, nL * 88 * 2], F32)
        Thalo = [Tl() for _ in range(max(nL, 1) * 88)]
        Tk = Tl()
        NTF = 8
        tfs = [sb("tf%d" % i, [128, TB], F32) for i in range(NTF)]
        Ttf = [Tl() for _ in range(NTF)]
        NTBF = 7
        tbs = [sb("tb%d" % i, [128, TB], BF16) for i in range(NTBF)]
        Ttb = [Tl() for _ in range(NTBF)]
        NAE = 4
        aes = [sb("ae%d" % i, [128, TB + 2], F32) for i in range(NAE)]
        Tae = [Tl() for _ in range(NAE)]
        smalls = sb("smalls", [128, 16], F32)
        Tsm = [Tl() for _ in range(4)]
        posi = sb("posi", [64, TB], I32)
        Tposi = Tl()
        cos2 = sb("cos2", [64, TB], F32)
        sin2 = sb("sin2", [64, TB], F32)
        Tcos, Tsin = Tl(), Tl()
        banks = [st.enter_context(nc.psum_tensor("bank%d" % i, [128, TB], F32)) for i in range(8)]
        Tb = [Tl() for _ in range(8)]
        block = st.enter_context(nc.Block())

        rr = {"ps": 0, "tf": 0, "tb": 0, "ae": 0, "w": 0, "ev": 0}
        pinned = set()

        def ps_alloc():
            while True:
                i = rr["ps"] % 8
                rr["ps"] += 1
                if i not in pinned:
                    return i

        def tf_alloc():
            i = rr["tf"] % NTF
            rr["tf"] += 1
            rr["tfi"] = i
            return tfs[i], Ttf[i]

        def tb_alloc():
            i = rr["tb"] % NTBF
            rr["tb"] += 1
            return tbs[i], Ttb[i]

        def ae_alloc():
            i = rr["ae"] % NAE
            rr["ae"] += 1
            return aes[i], Tae[i]

        def cc(i):
            return cstt[:, i:i + 1]

        def mm(out, lhsT, rhs, start, stop, reads, writes):
            P.op("pe", lambda e: e.matmul(out, lhsT, rhs, start=start, stop=stop), reads=reads, writes=writes, acc=True)

        def tr(out, in_, reads, writes):
            P.op("pe", lambda e: e.transpose(out, in_, ident[:]), reads=list(reads) + [Tk], writes=writes, acc=True)

        def act(out, in_, func, reads, writes, bias=None, scale=1.0, accum_out=None):
            kw = {}
            if bias is not None:
                kw["bias"] = bias
            if accum_out is not None:
                kw["accum_out"] = accum_out
            P.op("act", lambda e: e.activation(out, in_, func, scale=scale, **kw), reads=reads, writes=writes)

        def tcopy(eng, out, in_, reads, writes):
            P.op(eng, lambda e: e.tensor_copy(out, in_), reads=reads, writes=writes)

        def evac(out, in_, reads, writes):
            rr["ev"] += 1
            if rr["ev"] % 2 == 0:
                act(out, in_, AF.Copy, reads, writes)
            else:
                tcopy("dve", out, in_, reads, writes)

        def tt(eng, out, in0, in1, op, reads, writes):
            P.op(eng, lambda e: e.tensor_tensor(out=out, in0=in0, in1=in1, op=op), reads=reads, writes=writes)

        def ts(eng, out, in0, s1, s2, op0, op1, reads, writes):
            if s2 is None:
                P.op(eng, lambda e: e.tensor_scalar(out, in0, s1, None, op0=op0), reads=reads, writes=writes)
            else:
                P.op(eng, lambda e: e.tensor_scalar(out, in0, s1, s2, op0=op0, op1=op1), reads=reads, writes=writes)

        def stt(eng, out, in0, scalar, in1, op0, op1, reads, writes):
            P.op(eng, lambda e: e.scalar_tensor_tensor(out=out, in0=in0, scalar=scalar, in1=in1, op0=op0, op1=op1),
                 reads=reads, writes=writes)

        def recip(out, in_, reads, writes):
            P.op("dve", lambda e: e.reciprocal(out, in_), reads=reads, writes=writes)

        def memset(eng, ap, val, writes):
            P.op(eng, lambda e: e.memset(ap, val), writes=writes)

        def dma(q, out, in_, reads, writes, key):
            return P.op(q, lambda e: e.dma_start(out=out, in_=in_), reads=reads, writes=writes, key=key)

        def wget(name, ti):
            nt, kc, ct, _ = cat[name]
            i = rr["w"] % NWS
            rr["w"] += 1
            n = kc * ct
            dma("sp", wsl[i][:, 0:n], scr[name][ti], [], [Twsl[i]], "w%d" % i)
            return wsl[i], Twsl[i]

        dma("pool", cstt[:], dr["cst"][:, :], [], [Tk], "cst")
        dma("pool", ident[:], dr["cmat"][:, 0:128], [], [Tk], "cst")
        dma("pool", trif[:], dr["cmat"][:, 128:256], [], [Tk], "cst")
        tcopy("dve", trib[:], trif[:], [Tk], [Tk])
        memset("dve", ones_bf[:], 1.0, [Tk])
        memset("dve", on2048[:], 1.0 / 2048.0, [Tk])
        memset("dve", on512[:], 1.0 / 512.0, [Tk])
        memset("pool", halo[:], 0.0, Thalo)

        def rms_rstd(src, onm):
            n = len(src)
            N = src[0][0].shape[-1]
            b = ps_alloc()
            for c, (ap, T) in enumerate(src):
                sq, Tsq = tb_alloc()
                act(sq[:, 0:N], ap, AF.Square, [T], [Tsq])
                mm(banks[b][:, 0:N], onm[:], sq[:, 0:N], c == 0, c == n - 1, [Tsq, Tk], [Tb[b]])
            sd, Tsd = tf_alloc()
            act(sd[:, 0:N], banks[b][:, 0:N], AF.Sqrt, [Tb[b], Tk], [Tsd], bias=cc(EPSC))
            recip(sd[:, 0:N], sd[:, 0:N], [Tsd], [Tsd])
            return sd[:, 0:N], Tsd

        def rms_apply(src, dst, g0, rstd, Trstd):
            for c, ((ap, T), (dap, dT)) in enumerate(zip(src, dst)):
                stt("dve", dap, ap, cc(g0 + c), rstd, ALU.mult, ALU.mult, [T, Trstd, Tk], [dT])

        def xchunks():
            return [(xT[:, c, :], TxT[c]) for c in range(16)]

        def hchunks():
            return [(hT[:, c, :], ThT[c]) for c in range(16)]

        def norm_x_to_h(g0):
            rstd, Tr = rms_rstd(xchunks(), on2048)
            rms_apply(xchunks(), hchunks(), g0, rstd, Tr)

        memT = catT[:, :, :].rearrange("p c t -> p (c t)").bitcast(F32).rearrange("p (c m) -> p c m", m=256)
        TmemT = Tcat
        memn = reg2[:, 0:4096].rearrange("p (c m) -> p c m", m=256)
        Tmemn = [Treg2[c // 2] for c in range(16)]
        mk_store = {}
        for mt in range(2):
            for cg in range(4):
                s_, Ts_ = tf_alloc()
                dma("sp", s_[:, :], dr["mem"][mt * 128:(mt + 1) * 128, cg * 512:(cg + 1) * 512], [], [Ts_], "stg%d" % rr["tfi"])
                b = ps_alloc()
                for c in range(4):
                    tr(banks[b][:, c * 128:(c + 1) * 128], s_[:, c * 128:(c + 1) * 128], [Ts_], [Tb[b]])
                evac(memT[:, cg * 4:(cg + 1) * 4, mt * 128:(mt + 1) * 128],
                     banks[b][:, :].rearrange("p (c t) -> p c t", t=128), [Tb[b]], TmemT[cg * 4:(cg + 1) * 4])
        for li, (ty, l, j) in enumerate(ltypes):
            msrc = [(memT[:, c, :], TmemT[c]) for c in range(16)]
            mdst = [(memn[:, c, :], Tmemn[c]) for c in range(16)]
            rstd, Tr = rms_rstd(msrc, on2048)
            rms_apply(msrc, mdst, GMEM + l * 16, rstd, Tr)
            for wi in range(4):
                wt, Tw = wget("mkv%d" % l, wi)
                wv = wt[:, 0:4096].rearrange("p (k c) -> p k c", c=256)
                if wi < 2:
                    for hh in range(2):
                        h = wi * 2 + hh
                        b = ps_alloc()
                        for kc in range(16):
                            mm(banks[b][:, 0:256], wv[:, kc, hh * 128:(hh + 1) * 128], memn[:, kc, :], kc == 0, kc == 15,
                               [Tw, Tmemn[kc]], [Tb[b]])
                        evac(mkb[:, h * 256:(h + 1) * 256], banks[b][:, 0:256], [Tb[b]], [Tmk])
                else:
                    for mt in range(2):
                        b = ps_alloc()
                        for kc in range(16):
                            mm(banks[b][:, 0:256], memn[:, kc, mt * 128:(mt + 1) * 128], wv[:, kc, :], kc == 0, kc == 15,
                               [Tw, Tmemn[kc]], [Tb[b]])
                        c0 = 1024 + mt * 512 + (wi - 2) * 256
                        evac(mkb[:, c0:c0 + 256], banks[b][:, 0:256], [Tb[b]], [Tmk])
            mk_store[li] = dma("sp", scr_mk[li], mkb[:, :], [Tmk], [], "mkst")

        def mem_attention(li):
            sc = 128.0 ** -0.5
            P.op("sp", lambda e: e.dma_start(out=mkb[:, :], in_=scr_mk[li]), writes=[Tmk], key="mkld", extra=[mk_store[li]])
            for h in range(4):
                pts = []
                for mc in range(2):
                    b = ps_alloc()
                    mm(banks[b][:, :], mkb[:, h * 256 + mc * 128:h * 256 + (mc + 1) * 128], reg2c[:, 8 + h, :], True, True,
                       [Tmk, Treg2[8 + h]], [Tb[b]])
                    pt, Tpt = tb_alloc()
                    act(pt[:, :], banks[b][:, :], AF.Exp, [Tb[b]], [Tpt], scale=sc)
                    pts.append((pt, Tpt))
                bo = ps_alloc()
                bd = ps_alloc()
                for mc in range(2):
                    pt, Tpt = pts[mc]
                    mm(banks[bo][:, :], mkb[:, 1024 + mc * 512 + h * 128:1024 + mc * 512 + (h + 1) * 128], pt[:, :], mc == 0, mc == 1,
                       [Tmk, Tpt], [Tb[bo]])
                for mc in range(2):
                    pt, Tpt = pts[mc]
                    mm(banks[bd][:, :], ones_bf[:], pt[:, :], mc == 0, mc == 1, [Tk, Tpt], [Tb[bd]])
                rd, Trd = tf_alloc()
                recip(rd[:, :], banks[bd][:, :], [Tb[bd]], [Trd])
                tt("dve", catT[:, 12 + h, :], banks[bo][:, :], rd[:, :], ALU.mult, [Tb[bo], Trd], [Tcat[12 + h]])

        def out_proj(l):
            for wi in range(8):
                wt, Tw = wget("out%d" % l, wi)
                wv = wt[:, 0:4096].rearrange("p (k c) -> p k c", c=256)
                for hh in range(2):
                    oc = wi * 2 + hh
                    b = ps_alloc()
                    for kc in range(16):
                        mm(banks[b][:, :], wv[:, kc, hh * 128:(hh + 1) * 128], catT[:, kc, :], kc == 0, kc == 15,
                           [Tw, Tcat[kc]], [Tb[b]])
                    tt("dve", xT[:, oc, :], xT[:, oc, :], banks[b][:, :], ALU.add, [Tb[b], TxT[oc]], [TxT[oc]])

        FGROUPS = [(0, 8), (8, 8), (16, 8), (24, 8), (32, 8), (40, 4)]

        def ffn_up(li, l, gi):
            f0, n = FGROUPS[gi]
            gb = (gi % 2) * 8
            for jj in range(n):
                f = f0 + jj
                wt, Tw = wget("up%d" % l, f)
                wv = wt[:, 0:4096].rearrange("p (k c) -> p k c", c=256)
                res = []
                for part in range(2):
                    fc = f + part * NFC
                    b = ps_alloc()
                    for kc in range(16):
                        mm(banks[b][:, :], wv[:, kc, part * 128:(part + 1) * 128], hT[:, kc, :], kc == 0, kc == 15,
                           [Tw, ThT[kc]], [Tb[b]])
                    ae, Ta = ae_alloc()
                    hoff = (li * 88 + fc) * 2
                    Th = Thalo[li * 88 + fc]
                    tcopy("pool", ae[:, 0:2], halo[:, hoff:hoff + 2], [Th], [Ta])
                    act(ae[:, 2:TB + 2], banks[b][:, :], AF.Copy, [Tb[b]], [Ta])
                    tcopy("pool", halo[:, hoff:hoff + 2], ae[:, TB:TB + 2], [Ta], [Th])
                    t1, T1 = tf_alloc()
                    w0 = cc(CW + (l * 3 + 0) * 88 + fc)
                    w1 = cc(CW + (l * 3 + 1) * 88 + fc)
                    w2 = cc(CW + (l * 3 + 2) * 88 + fc)
                    act(t1[:, :], banks[b][:, :], AF.Identity, [Tb[b], Tk], [T1], bias=cc(CB + l * 88 + fc), scale=w2)
                    stt("dve", t1[:, :], ae[:, 1:TB + 1], w1, t1[:, :], ALU.mult, ALU.add, [Ta, Tk, T1], [T1])
                    stt("dve", t1[:, :], ae[:, 0:TB], w0, t1[:, :], ALU.mult, ALU.add, [Ta, Tk, T1], [T1])
                    res.append((t1, T1))
                (tg, Tg), (tv, Tv) = res
                act(tg[:, :], tg[:, :], AF.Silu, [Tg], [Tg])
                tt("dve", catT[:, gb + jj, :], tg[:, :], tv[:, :], ALU.mult, [Tg, Tv], [Tcat[gb + jj]])

        def ffn_down(li, l, gi):
            f0, n = FGROUPS[gi]
            gb = (gi % 2) * 8
            for q in range(4):
                if n == 8:
                    wt, Tw = wget("down%d" % l, gi * 4 + q)
                else:
                    wt, Tw = wget("downt%d" % l, q)
                wv = wt[:, 0:n * 512].rearrange("p (k c) -> p k c", c=512)
                for o4 in range(4):
                    oc = q * 4 + o4
                    b = ps_alloc()
                    for kc in range(n):
                        mm(banks[b][:, :], wv[:, kc, o4 * 128:(o4 + 1) * 128], catT[:, gb + kc, :], kc == 0, kc == n - 1,
                           [Tw, Tcat[gb + kc]], [Tb[b]])
                    tt("dve", xT[:, oc, :], xT[:, oc, :], banks[b][:, :], ALU.add, [Tb[b], TxT[oc]], [TxT[oc]])

        def ffn(li, l):
            norm_x_to_h(GFFN + l * 16)
            ffn_up(li, l, 0)
            for gi in range(len(FGROUPS)):
                if gi + 1 < len(FGROUPS):
                    ffn_up(li, l, gi + 1)
                ffn_down(li, l, gi)

        def mixer_A(li, l, ai):
            norm_x_to_h(GMIX + l * 16)
            for wi in range(6):
                wt, Tw = wget("ina%d" % l, wi)
                wv = wt[:, 0:4096].rearrange("p (k c) -> p k c", c=256)
                for t4 in range(4):
                    b = ps_alloc()
                    for kc in range(16):
                        mm(banks[b][:, 0:256], hT[:, kc, t4 * 128:(t4 + 1) * 128], wv[:, kc, :], kc == 0, kc == 15,
                           [Tw, ThT[kc]], [Tb[b]])
                    act(reg2v[:, t4, wi * 256:(wi + 1) * 256], banks[b][:, 0:256], AF.Gelu, [Tb[b]], Treg2[t4 * 3:t4 * 3 + 3])
            for t4 in range(4):
                Tv3 = Treg2[t4 * 3:t4 * 3 + 3]
                ss = smalls[:, t4:t4 + 1]
                memset("pool", ss, 0.0, [Tsm[t4]])
                act(catT[:, 12:15, :].rearrange("p c t -> p (c t)"), reg2v[:, t4, :], AF.Square, Tv3, Tcat[12:15] + [Tsm[t4]], accum_out=ss)
                act(ss, ss, AF.Sqrt, [Tsm[t4], Tk], [Tsm[t4]], bias=cc(EPSC), scale=1.0 / 1536.0)
                recip(ss, ss, [Tsm[t4]], [Tsm[t4]])
                ts("dve", reg2v[:, t4, :], reg2v[:, t4, :], ss, None, ALU.mult, None, Tv3 + [Tsm[t4]], Tv3)
            dma("sp", wspb[:, :], scr["sp%d" % l][0], [], [Twsp], "wsp")
            dma("sp", bspt[:, :], dr["bsp"][:, ai * 1536:(ai + 1) * 1536], [], [Tbsp], "bsp")
            wsp = wspb
            for wi in range(6):
                wt, Tw = wget("ina%d" % l, 6 + wi)
                wv = wt[:, 0:4096].rearrange("p (k c) -> p k c", c=256)
                for hh in range(2):
                    g = wi * 2 + hh
                    bu = ps_alloc()
                    for kc in range(16):
                        mm(banks[bu][:, :], wv[:, kc, hh * 128:(hh + 1) * 128], hT[:, kc, :], kc == 0, kc == 15,
                           [Tw, ThT[kc]], [Tb[bu]])
                    ug, Tug = tf_alloc()
                    act(ug[:, :], banks[bu][:, :], AF.Gelu, [Tb[bu]], [Tug])
                    bs = ps_alloc()
                    for t4 in range(4):
                        mm(banks[bs][:, t4 * 128:(t4 + 1) * 128], reg2v[:, t4, g * 128:(g + 1) * 128], wsp[:, g * 128:(g + 1) * 128],
                           True, True, Treg2[t4 * 3:t4 * 3 + 3] + [Twsp], [Tb[bs]])
                    sv, Tsv = tf_alloc()
                    for t4 in range(4):
                        stt("dve", sv[:, t4 * 128:(t4 + 1) * 128], banks[bs][:, t4 * 128:(t4 + 1) * 128], cc(GV + l * 12 + g),
                            bspt[:, g * 128:(g + 1) * 128], ALU.mult, ALU.add, [Tb[bs], Tk, Tbsp], [Tsv])
                    tt("dve", catT[:, g, :], sv[:, :], ug[:, :], ALU.mult, [Tsv, Tug], [Tcat[g]])
            for wi in range(2):
                wt, Tw = wget("ina%d" % l, 12 + wi)
                wv = wt[:, 0:4096].rearrange("p (k c) -> p k c", c=256)
                for hh in range(2):
                    h = wi * 2 + hh
                    b = ps_alloc()
                    for kc in range(16):
                        mm(banks[b][:, :], wv[:, kc, hh * 128:(hh + 1) * 128], hT[:, kc, :], kc == 0, kc == 15,
                           [Tw, ThT[kc]], [Tb[b]])
                    evac(reg2c[:, 8 + h, :], banks[b][:, :], [Tb[b]], [Treg2[8 + h]])
            mem_attention(li)
            out_proj(l)

        spbuf = {}

        def wsl_sp(l):
            if l not in spbuf:
                t_ = sb("wsp%d" % l, [128, 1536], BF16)
                T_ = Tl()
                dma("sp", t_[:, :], scr["sp%d" % l][0], [], [T_], "wsp%d" % l)
                spbuf[l] = (t_, T_)
            return spbuf[l]

        def rope_tables(t):
            dma("sp", posi[:, :], dr["posb"][:, t * TB:(t + 1) * TB], [], [Tposi], "posi")
            for which, (dst, Td) in enumerate(((sin2, Tsin), (cos2, Tcos))):
                a, Ta = tf_alloc()
                u, Tu = tf_alloc()
                av, uv = a[0:64, :], u[0:64, :]
                tcopy("dve", av, posi[:, :], [Tposi], [Ta])
                ts("dve", av, av, cstt[0:64, INV:INV + 1], None, ALU.mult, None, [Ta, Tk], [Ta])
                if which == 1:
                    ts("dve", av, av, float(np.pi / 2), None, ALU.add, None, [Ta], [Ta])
                kt_, Tkt = tf_alloc()
                kiv = kt_[0:64, :].bitcast(I32)
                ts("dve", kiv, av, float(1.0 / (2 * np.pi)), None, ALU.mult, None, [Ta], [Tkt])
                tcopy("dve", uv, kiv, [Tkt], [Tu])
                stt("dve", av, uv, float(-C1), av, ALU.mult, ALU.add, [Tu, Ta], [Ta])
                stt("dve", av, uv, float(-C2), av, ALU.mult, ALU.add, [Tu, Ta], [Ta])
                ts("dve", av, av, float(PI_SAFE), float(-PI_SAFE), ALU.min, ALU.max, [Ta], [Ta])
                if which == 0:
                    act(dst[:, :], av, AF.Sin, [Ta, Tk], [Td], scale=cstt[0:64, SGN:SGN + 1])
                else:
                    act(dst[:, :], av, AF.Sin, [Ta], [Td])

        def rope_apply(ba, bb, dst, reads, writes):
            k1, T1 = tf_alloc()
            k2, T2 = tf_alloc()
            tt("dve", k1[0:64, :], ba, cos2[:, :], ALU.mult, reads + [Tcos], [T1])
            tt("dve", k2[0:64, :], bb, sin2[:, :], ALU.mult, reads + [Tsin], [T2])
            tt("dve", dst, k1[0:64, :], k2[0:64, :], ALU.add, [T1, T2], writes)

        def kv_build(t):
            norm_x_to_h(GKV)
            kvf = []
            for wi in range(4):
                wt, Tw = wget("kva", wi)
                wv = wt[:, 0:2048].rearrange("p (k c) -> p k c", c=128)
                b = ps_alloc()
                for kc in range(16):
                    mm(banks[b][:, :], wv[:, kc, :], hT[:, kc, :], kc == 0, kc == 15, [Tw, ThT[kc]], [Tb[b]])
                f_, Tf_ = tf_alloc()
                evac(f_[:, :], banks[b][:, :], [Tb[b]], [Tf_])
                kvf.append((f_[:, :], Tf_))
            wt, Tw = wget("kva", 4)
            wv = wt[:, 0:2048].rearrange("p (k c) -> p k c", c=128)
            ba, bb = ps_alloc(), ps_alloc()
            for kc in range(16):
                mm(banks[ba][0:64, :], wv[:, kc, 0:64], hT[:, kc, :], kc == 0, kc == 15, [Tw, ThT[kc]], [Tb[ba]])
            for kc in range(16):
                mm(banks[bb][0:64, :], wv[:, kc, 64:128], hT[:, kc, :], kc == 0, kc == 15, [Tw, ThT[kc]], [Tb[bb]])
            rope_apply(banks[ba][0:64, :], banks[bb][0:64, :], kropeT[:, t * TB:(t + 1) * TB], [Tb[ba], Tb[bb]], [Tkrope[t]])
            rstd, Tr = rms_rstd(kvf, on512)
            dst = [(ckvT[:, rc, t * TB:(t + 1) * TB], TckvT[rc][t]) for rc in range(4)]
            rms_apply(kvf, dst, GKVL, rstd, Tr)

        def mixer_B(li, l, j, t):
            norm_x_to_h(GMIX + l * 16)
            qlf = []
            for wi in range(4):
                wt, Tw = wget("inb%d" % j, wi)
                wv = wt[:, 0:4096].rearrange("p (k c) -> p k c", c=256)
                for hh in range(2):
                    c = wi * 2 + hh
                    b = ps_alloc()
                    for kc in range(16):
                        mm(banks[b][:, :], wv[:, kc, hh * 128:(hh + 1) * 128], hT[:, kc, :], kc == 0, kc == 15,
                           [Tw, ThT[kc]], [Tb[b]])
                    if c < 4:
                        f_, Tf_ = tf_alloc()
                        evac(f_[:, :], banks[b][:, :], [Tb[b]], [Tf_])
                        qlf.append((f_[:, :], Tf_))
                    else:
                        evac(reg2c[:, 8 + (c - 4), :], banks[b][:, :], [Tb[b]], [Treg2[8 + c - 4]])
            rstd, Tr = rms_rstd(qlf, on512)
            qln = [(qlnb[:, rc, :], Tqln[rc]) for rc in range(4)]
            rms_apply(qlf, qln, GQL + j * 4, rstd, Tr)
            mem_attention(li)
            sc = 192.0 ** -0.5
            for th in range(3):
                wt, Tw = wget("uqn%d" % j, th)
                wv = wt[:, 0:2048].rearrange("p (k c) -> p k c", c=512)
                for hh in range(4):
                    b = ps_alloc()
                    for kc in range(4):
                        mm(banks[b][:, :], wv[:, kc, hh * 128:(hh + 1) * 128], qln[kc][0], kc == 0, kc == 3,
                           [Tw, qln[kc][1]], [Tb[b]])
                    evac(reg2c[:, hh, :], banks[b][:, :], [Tb[b]], [Treg2[hh]])
                wt, Tw = wget("uqr%d" % j, th)
                wv = wt[:, 0:2048].rearrange("p (k c) -> p k c", c=512)
                for hh in range(4):
                    ba, bb = ps_alloc(), ps_alloc()
                    for kc in range(4):
                        mm(banks[ba][0:64, :], wv[:, kc, hh * 128:hh * 128 + 64], qln[kc][0], kc == 0, kc == 3,
                           [Tw, qln[kc][1]], [Tb[ba]])
                    for kc in range(4):
                        mm(banks[bb][0:64, :], wv[:, kc, hh * 128 + 64:hh * 128 + 128], qln[kc][0], kc == 0, kc == 3,
                           [Tw, qln[kc][1]], [Tb[bb]])
                    rope_apply(banks[ba][0:64, :], banks[bb][0:64, :], reg2c[0:64, 4 + hh, :], [Tb[ba], Tb[bb]], [Treg2[4 + hh]])
                wk, Twk = wget("uk%d" % j, th)
                wkv = wk[:, 0:2048].rearrange("p (k c) -> p k c", c=512)
                wu, Twu = wget("uv%d" % j, th)
                wuv = wu[:, 0:2048].rearrange("p (k c) -> p k c", c=512)
                for hh in range(4):
                    h = th * 4 + hh
                    bacc, bden = ps_alloc(), ps_alloc()
                    pinned.add(bacc)
                    pinned.add(bden)
                    first = True
                    for kg in range(t + 1):
                        b = ps_alloc()
                        for rc in range(4):
                            mm(banks[b][:, :], wkv[:, rc, hh * 128:(hh + 1) * 128], ckvT[:, rc, kg * TB:(kg + 1) * TB], rc == 0, rc == 3,
                               [Twk, TckvT[rc][kg]], [Tb[b]])
                        kh, Tkh = tb_alloc()
                        evac(kh[:, :], banks[b][:, :], [Tb[b]], [Tkh])
                        b = ps_alloc()
                        for jx in range(4):
                            for rc in range(4):
                                mm(banks[b][:, jx * 128:(jx + 1) * 128], ckvT[:, rc, kg * TB + jx * 128:kg * TB + (jx + 1) * 128],
                                   wuv[:, rc, hh * 128:(hh + 1) * 128], rc == 0, rc == 3, [Twu, TckvT[rc][kg]], [Tb[b]])
                        vh, Tvh = tb_alloc()
                        evac(vh[:, :], banks[b][:, :], [Tb[b]], [Tvh])
                        for jx in range(4):
                            kt = kg * 4 + jx
                            c0 = jx * 128 if kg == t else 0
                            b = ps_alloc()
                            mm(banks[b][:, c0:TB], kh[:, jx * 128:(jx + 1) * 128], reg2c[:, hh, c0:TB], True, False,
                               [Tkh, Treg2[hh]], [Tb[b]])
                            mm(banks[b][:, c0:TB], kropeT[:, kt * 128:(kt + 1) * 128], reg2c[0:64, 4 + hh, c0:TB], False, True,
                               [Tkrope[kg], Treg2[4 + hh]], [Tb[b]])
                            pt, Tpt = tb_alloc()
                            act(pt[:, c0:TB], banks[b][:, c0:TB], AF.Exp, [Tb[b]], [Tpt], scale=sc)
                            if kg == t:
                                tt("pool", pt[:, c0:c0 + 128], pt[:, c0:c0 + 128], trib[:], ALU.mult, [Tpt, Tk], [Tpt])
                            mm(banks[bacc][:, c0:TB], vh[:, jx * 128:(jx + 1) * 128], pt[:, c0:TB], first, False, [Tvh, Tpt], [Tb[bacc]])
                            mm(banks[bden][:, c0:TB], ones_bf[:], pt[:, c0:TB], first, False, [Tk, Tpt], [Tb[bden]])
                            first = False
                    rd, Trd = tf_alloc()
                    recip(rd[:, :], banks[bden][:, :], [Tb[bden]], [Trd])
                    tt("dve", catT[:, h, :], banks[bacc][:, :], rd[:, :], ALU.mult, [Tb[bacc], Trd], [Tcat[h]])
                    pinned.discard(bacc)
                    pinned.discard(bden)
            out_proj(l)

        final_stores = []
        for t in range(NB):
            for t4 in range(4):
                for cg in range(4):
                    s_, Ts_ = tf_alloc()
                    r0 = t * TB + t4 * 128
                    dma("sp", s_[:, :], dr["x"][r0:r0 + 128, cg * 512:(cg + 1) * 512], [], [Ts_], "stg%d" % rr["tfi"])
                    b = ps_alloc()
                    for c in range(4):
                        tr(banks[b][:, c * 128:(c + 1) * 128], s_[:, c * 128:(c + 1) * 128], [Ts_], [Tb[b]])
                    evac(xT[:, cg * 4:(cg + 1) * 4, t4 * 128:(t4 + 1) * 128],
                         banks[b][:, :].rearrange("p (c t) -> p c t", t=128), [Tb[b]], TxT[cg * 4:(cg + 1) * 4])
            if hasB:
                rope_tables(t)
            ai = 0
            kv_done = False
            for li, (ty, l, j) in enumerate(ltypes):
                if ty == "A":
                    mixer_A(li, l, ai)
                    ai += 1
                else:
                    if not kv_done:
                        kv_build(t)
                        kv_done = True
                    mixer_B(li, l, j, t)
                ffn(li, l)
            rstd0, Tr0 = rms_rstd(xchunks(), on2048)
            tcopy("dve", rstd_fin[:, :], rstd0, [Tr0], [Trfin])
            rstd, Tr = rstd_fin[:, :], Trfin
            for cg in range(4):
                ys = []
                for c4 in range(4):
                    c = cg * 4 + c4
                    y_, Ty_ = tf_alloc()
                    stt("dve", y_[:, :], xT[:, c, :], cc(GFIN + c), rstd, ALU.mult, ALU.mult, [TxT[c], Tr, Tk], [Ty_])
                    ys.append((y_, Ty_))
                for t4 in range(4):
                    b = ps_alloc()
                    for c4 in range(4):
                        tr(banks[b][:, c4 * 128:(c4 + 1) * 128], ys[c4][0][:, t4 * 128:(t4 + 1) * 128], [ys[c4][1]], [Tb[b]])
                    o_, To_ = tf_alloc()
                    evac(o_[:, :], banks[b][:, :], [Tb[b]], [To_])
                    r0 = t * TB + t4 * 128
                    final_stores.append(dma("pool", y[r0:r0 + 128, cg * 512:(cg + 1) * 512], o_[:, :], [To_], [],
                                            "ost%d" % rr["tfi"]))
        lastk = {}
        for o in final_stores:
            lastk[o.key] = o
        P.wait("pool", list(lastk.values()))
        P.emit(block)
    return nc


LT_FULL = [("A", 0, 0), ("A", 1, 1), ("B", 2, 0), ("B", 3, 1)]


def host_inputs(inp, ltypes, S, b):
    f = lambda a: np.ascontiguousarray(np.asarray(a, dtype=np.float32))
    m = {}
    m["x"] = f(inp["x"][b, :S])
    m["mem"] = f(inp["mem"][b])
    m["posb"] = np.ascontiguousarray(np.broadcast_to(np.asarray(inp["positions"][b, :S], dtype=np.int32)[None, :], (64, S)))
    cst = np.zeros((128, NC_), np.float32)

    def pm(v):
        return np.asarray(v, np.float32).reshape(-1, 128).T

    for l in range(4):
        cst[:, GMIX + l * 16:GMIX + (l + 1) * 16] = pm(inp["g_mix"][l])
        cst[:, GFFN + l * 16:GFFN + (l + 1) * 16] = pm(inp["g_ffn"][l])
        cst[:, GMEM + l * 16:GMEM + (l + 1) * 16] = pm(inp["g_mem"][l])
        for k in range(3):
            cst[:, CW + (l * 3 + k) * 88:CW + (l * 3 + k + 1) * 88] = pm(inp["conv_w"][l, k])
        cst[:, CB + l * 88:CB + (l + 1) * 88] = pm(inp["conv_b"][l])
    cst[:, GFIN:GFIN + 16] = pm(inp["g_final"])
    cst[:, GKV:GKV + 16] = pm(inp["g_kv"])
    cst[:, GKVL:GKVL + 4] = pm(inp["g_kv_lat"])
    for j in range(2):
        cst[:, GQL + j * 4:GQL + (j + 1) * 4] = pm(inp["g_q_lat"][j])
        cst[:, GV + j * 12:GV + (j + 1) * 12] = pm(inp["g_v"][j])
    inv = (1.0 / (10000.0 ** (np.arange(0, 64, 2, dtype=np.float32) / np.float32(64)))).astype(np.float32)
    cst[0:32, INV] = inv
    cst[32:64, INV] = inv
    cst[0:32, SGN] = -1.0
    cst[32:64, SGN] = 1.0
    cst[:, EPSC] = EPS
    m["cst"] = cst
    cm = np.zeros((128, 256), np.float32)
    cm[:, 0:128] = np.eye(128, dtype=np.float32)
    cm[:, 128:256] = np.triu(np.ones((128, 128), np.float32))
    m["cmat"] = cm
    nA = sum(1 for t in ltypes if t[0] == "A")
    bsp = np.zeros((128, max(nA, 1) * 1536), np.float32)
    ai = 0
    for (ty, l, j) in ltypes:
        if ty == "A":
            bsp[:, ai * 1536:(ai + 1) * 1536] = np.asarray(inp["b_sp"][l], np.float32).reshape(1, 1536)
            ai += 1
            m["w_in_a%d" % l] = f(inp["w_in_a"][l])
            m["w_spt%d" % l] = f(np.transpose(np.asarray(inp["w_sp"][l]), (2, 0, 1)).reshape(128, 1536))
        else:
            m["w_in_b%d" % j] = f(inp["w_in_b"][j])
            wq = np.asarray(inp["w_uq"][j], np.float32).reshape(512, 12, 192)
            m["w_uqn%d" % j] = f(wq[:, :, 0:128].reshape(512, 1536))
            m["w_uqr%d" % j] = f(np.concatenate([wq[:, :, 128:192], wq[:, :, 160:192], wq[:, :, 128:160]], axis=2).reshape(512, 1536))
            m["w_uk%d" % j] = f(np.asarray(inp["w_uk"][j]).reshape(512, 1536))
            m["w_uv%d" % j] = f(np.asarray(inp["w_uv"][j]).reshape(512, 1536))
        m["w_out%d" % l] = f(inp["w_out"][l])
        m["w_up%d" % l] = f(inp["w_ffn_up"][l])
        m["w_down%d" % l] = f(inp["w_ffn_down"][l])
        m["w_mkv%d" % l] = f(inp["w_mem_kv"][l])
    m["bsp"] = bsp
    if any(t[0] == "B" for t in ltypes):
        wk = np.asarray(inp["w_kv_a"], np.float32)
        m["w_kva"] = f(np.concatenate([wk[:, 0:576], wk[:, 544:576], wk[:, 512:544]], axis=1))
    return m


_CACHE = {}


def run(inp, ltypes, S, ncores=8):
    key = (S, tuple(ltypes))
    if key not in _CACHE:
        _CACHE[key] = build(S, ltypes)
    nc = _CACHE[key]
    shared = None
    in_maps = []
    for b in range(ncores):
        mb = host_inputs(inp, ltypes, S, b) if shared is None else dict(shared)
        if shared is None:
            shared = mb
        else:
            f = lambda a: np.ascontiguousarray(np.asarray(a, dtype=np.float32))
            mb["x"] = f(inp["x"][b, :S])
            mb["mem"] = f(inp["mem"][b])
            mb["posb"] = np.ascontiguousarray(np.broadcast_to(np.asarray(inp["positions"][b, :S], dtype=np.int32)[None, :], (64, S)))
        in_maps.append(mb)
    res = run_bass_kernel_spmd(nc, in_maps, core_ids=list(range(ncores)))
    return np.stack([np.asarray(r["y"], dtype=np.float32) for r in res.results], axis=0)


def kernel(**inputs):
    return run(inputs, LT_FULL, 4096, 8)
```

```python
import contextlib
import numpy as np
import concourse.bass as bass
import concourse.mybir as mybir
from concourse.bass_utils import run_bass_kernel_spmd

F32 = mybir.dt.float32
BF16 = mybir.dt.bfloat16
I32 = mybir.dt.int32
ALU = mybir.AluOpType
AF = mybir.ActivationFunctionType

D = 2048
DFF = 5632
NFC = 44
TB = 512
EPS = 1e-6
ENGS = ("pe", "act", "dve", "pool", "sp")
EPOCH = 30000

GMIX, GFFN, GMEM, GFIN, GKV, GKVL, GQL, GV = 0, 64, 128, 192, 208, 224, 228, 236
CW = 260
CB = CW + 4 * 3 * 88
INV = CB + 4 * 88
SGN = INV + 1
EPSC = INV + 2
NC_ = INV + 4
PI_SAFE = 3.1415925
C1 = 6.28125
C2 = 2.0 * np.pi - 6.28125


class Tl:
    __slots__ = ("w", "r")

    def __init__(self):
        self.w = {}
        self.r = {}


class Op:
    __slots__ = ("eng", "fn", "deps", "needed", "sem", "val", "key", "stream")


class Prog:
    def __init__(self, nc, stack):
        self.nc = nc
        self.stack = stack
        self.ops = {e: [] for e in ENGS}

    def op(self, eng, fn, reads=(), writes=(), key=None, acc=False, extra=()):
        o = Op()
        o.eng = eng
        o.fn = fn
        o.key = key
        o.needed = False
        o.sem = None
        o.val = 0
        st = o.stream = key if key is not None else eng
        deps = set(extra)
        for t in reads:
            for s, p in t.w.items():
                if s == st and key is not None:
                    continue
                deps.add(p)
        for t in writes:
            for s, p in t.w.items():
                if s == st and (key is not None or acc):
                    continue
                deps.add(p)
            for s, p in t.r.items():
                if s == st and key is not None:
                    continue
                deps.add(p)
        for p in deps:
            p.needed = True
        o.deps = deps
        for t in reads:
            t.r[st] = o
        for t in writes:
            t.w = {st: o}
            t.r = {}
        self.ops[eng].append(o)
        return o

    def wait(self, eng, ops):
        return self.op(eng, None, extra=[o for o in ops if o is not None])

    def emit(self, block):
        nc, stack = self.nc, self.stack
        keysem, keycnt = {}, {}
        for e in ENGS:
            cnt, sem, nsem = 0, None, 0
            for o in self.ops[e]:
                if o.key is not None:
                    if o.key not in keysem:
                        keysem[o.key] = stack.enter_context(nc.semaphore("k_" + str(o.key)))
                        keycnt[o.key] = 0
                    keycnt[o.key] += 1
                    o.sem = keysem[o.key]
                    o.val = 16 * keycnt[o.key]
                elif o.needed and o.fn is not None:
                    if sem is None or cnt >= EPOCH:
                        sem = stack.enter_context(nc.semaphore("e_%s_%d" % (e, nsem)))
                        nsem += 1
                        cnt = 0
                    cnt += 1
                    o.sem = sem
                    o.val = cnt

        def run(e, eng):
            seen = {}
            for o in self.ops[e]:
                waits = {}
                for p in o.deps:
                    sid = id(p.sem)
                    if seen.get(sid, 0) >= p.val:
                        continue
                    if sid not in waits or waits[sid][1] < p.val:
                        waits[sid] = (p.sem, p.val)
                wl = list(waits.values())
                for s, v in wl:
                    seen[id(s)] = v
                if o.fn is None:
                    for s, v in wl:
                        eng.wait_ge(s, v)
                    continue
                for s, v in wl[1:]:
                    eng.wait_ge(s, v)
                ins = o.fn(eng)
                if wl:
                    ins._wait_ge(wl[0][0], wl[0][1])
                if o.key is not None:
                    ins.then_inc(o.sem, 16)
                elif o.sem is not None:
                    ins.then_inc(o.sem, 1)

        block.tensor(lambda eng: run("pe", eng))
        block.scalar(lambda eng: run("act", eng))
        block.vector(lambda eng: run("dve", eng))
        block.gpsimd(lambda eng: run("pool", eng))
        block.sync(lambda eng: run("sp", eng))


def wcat(ltypes):
    cat = {}
    for (ty, l, j) in ltypes:
        if ty == "A":
            def ina(i, l=l):
                if i < 6:
                    c0 = 1536 + 256 * i
                elif i < 12:
                    c0 = 256 * (i - 6)
                else:
                    c0 = 3072 + 256 * (i - 12)
                return [("w_in_a%d" % l, 0, c0, 256, 0)]
            cat["ina%d" % l] = (14, 16, 256, ina)
        else:
            cat["inb%d" % j] = (4, 16, 256, lambda i, j=j: [("w_in_b%d" % j, 0, 256 * i, 256, 0)])
            for nm in ("uqn", "uqr", "uk", "uv"):
                cat["%s%d" % (nm, j)] = (3, 4, 512, lambda i, j=j, nm=nm: [("w_%s%d" % (nm, j), 0, 512 * i, 512, 0)])
        cat["out%d" % l] = (8, 16, 256, lambda i, l=l: [("w_out%d" % l, 0, 256 * i, 256, 0)])
        cat["up%d" % l] = (NFC, 16, 256, lambda i, l=l: [("w_up%d" % l, 0, 128 * i, 128, 0),
                                                         ("w_up%d" % l, 0, DFF + 128 * i, 128, 128)])
        cat["down%d" % l] = (20, 8, 512, lambda i, l=l: [("w_down%d" % l, 8 * (i // 4), 512 * (i % 4), 512, 0)])
        cat["downt%d" % l] = (4, 4, 512, lambda i, l=l: [("w_down%d" % l, 40, 512 * i, 512, 0)])
        cat["mkv%d" % l] = (4, 16, 256, lambda i, l=l: [("w_mkv%d" % l, 0, 256 * i, 256, 0)])
    if any(t[0] == "B" for t in ltypes):
        cat["kva"] = (5, 16, 128, lambda i: [("w_kva", 0, 128 * i, 128, 0)])
    return cat


def wshapes(ltypes):
    sh = {}
    for (ty, l, j) in ltypes:
        if ty == "A":
            sh["w_in_a%d" % l] = (D, 3584)
            sh["w_spt%d" % l] = (128, 1536)
        else:
            sh["w_in_b%d" % j] = (D, 1024)
            for nm in ("uqn", "uqr", "uk", "uv"):
                sh["w_%s%d" % (nm, j)] = (512, 1536)
        sh["w_out%d" % l] = (D, D)
        sh["w_up%d" % l] = (D, 2 * DFF)
        sh["w_down%d" % l] = (DFF, D)
        sh["w_mkv%d" % l] = (D, 1024)
    if any(t[0] == "B" for t in ltypes):
        sh["w_kva"] = (D, 640)
    return sh


def build(S, ltypes):
    NB = S // TB
    nA = sum(1 for t in ltypes if t[0] == "A")
    hasB = any(t[0] == "B" for t in ltypes)
    nc = bass.Bass("TRN2", target_bir_lowering=False)
    dr = {}
    dr["x"] = nc.dram_tensor("x", [S, D], F32, kind="ExternalInput").ap()
    dr["mem"] = nc.dram_tensor("mem", [256, D], F32, kind="ExternalInput").ap()
    dr["posb"] = nc.dram_tensor("posb", [64, S], I32, kind="ExternalInput").ap()
    dr["cst"] = nc.dram_tensor("cst", [128, NC_], F32, kind="ExternalInput").ap()
    dr["cmat"] = nc.dram_tensor("cmat", [128, 256], F32, kind="ExternalInput").ap()
    dr["bsp"] = nc.dram_tensor("bsp", [128, max(nA, 1) * 1536], F32, kind="ExternalInput").ap()
    for nm, shp in wshapes(ltypes).items():
        dr[nm] = nc.dram_tensor(nm, list(shp), F32, kind="ExternalInput").ap()
    y = nc.dram_tensor("y", [S, D], F32, kind="ExternalOutput").ap()
    cat = wcat(ltypes)
    scr = {}
    for nm, (nt, kc, ct, _) in cat.items():
        scr[nm] = nc.dram_tensor("scr_" + nm, [nt, 128, kc * ct], BF16).ap()
    for (ty, l, j) in ltypes:
        if ty == "A":
            scr["sp%d" % l] = nc.dram_tensor("scr_sp%d" % l, [1, 128, 1536], BF16).ap()
    scr_mk = nc.dram_tensor("scr_mk", [max(1, len(ltypes)), 128, 2048], BF16).ap()

    with contextlib.ExitStack() as st:
        P = Prog(nc, st)
        NS = 6
        stg = [st.enter_context(nc.sbuf_tensor("stg%d" % i, [128, 4096], F32)) for i in range(NS)]
        cvt = [st.enter_context(nc.sbuf_tensor("cvt%d" % i, [128, 4096], BF16)) for i in range(NS)]
        trif = st.enter_context(nc.sbuf_tensor("trif", [128, 128], F32))
        Ttri = Tl()
        Ts = [Tl() for _ in range(NS)]
        Tc = [Tl() for _ in range(NS)]
        block = st.enter_context(nc.Block())
        P.op("sp", lambda e: e.dma_start(out=trif[:], in_=dr["cmat"][:, 128:256]), writes=[Ttri], key="tri")
        cnt = [0]
        stores = []

        def convert(dst_ap, n, pieces, mask=False):
            i = cnt[0] % NS
            cnt[0] += 1
            for (dst_v, src_v) in pieces(stg[i]):
                P.op("sp", (lambda d, s: lambda e: e.dma_start(out=d, in_=s))(dst_v, src_v), writes=[Ts[i]], key="ps%d" % i)
            if mask:
                a = stg[i][:, 0:n].rearrange("p (g t) -> p g t", t=128)
                o = cvt[i][:, 0:n].rearrange("p (g t) -> p g t", t=128)
                for g in range(12):
                    P.op("dve", (lambda o, a: lambda e: e.tensor_tensor(out=o, in0=a, in1=trif[:], op=ALU.mult))(o[:, g, :], a[:, g, :]),
                         reads=[Ts[i], Ttri], writes=[Tc[i]])
            elif cnt[0] % 3 == 0:
                P.op("act", (lambda i: lambda e: e.activation(cvt[i][:, 0:n], stg[i][:, 0:n], AF.Copy))(i), reads=[Ts[i]], writes=[Tc[i]])
            else:
                P.op("dve", (lambda i: lambda e: e.tensor_copy(cvt[i][:, 0:n], stg[i][:, 0:n]))(i), reads=[Ts[i]], writes=[Tc[i]])
            stores.append(P.op("pool", (lambda i: lambda e: e.dma_start(out=dst_ap, in_=cvt[i][:, 0:n]))(i), reads=[Tc[i]], key="pc%d" % i))

        for nm, (nt, kc, ct, pf) in cat.items():
            for ti in range(nt):
                def pieces(stg_t, ti=ti, kc=kc, ct=ct, pf=pf):
                    res = []
                    sv = stg_t[:, 0:kc * ct].rearrange("p (k c) -> p k c", c=ct)
                    for (src, r0, c0, ncol, d0) in pf(ti):
                        srcv = dr[src].rearrange("(k p) n -> p k n", p=128)[:, r0:r0 + kc, c0:c0 + ncol]
                        res.append((sv[:, :, d0:d0 + ncol], srcv))
                    return res
                convert(scr[nm][ti], kc * ct, pieces)
        for (ty, l, j) in ltypes:
            if ty == "A":
                convert(scr["sp%d" % l][0], 1536, (lambda l: lambda stg_t: [(stg_t[:, 0:1536], dr["w_spt%d" % l][:, :])])(l), mask=True)
        P.wait("pool", stores[-2 * NS:])
        P.emit(block)

    with contextlib.ExitStack() as st:
        P = Prog(nc, st)

        def sb(name, shape, dt):
            return st.enter_context(nc.sbuf_tensor(name, shape, dt))

        xT = sb("xT", [128, 16, TB], F32)
        TxT = [Tl() for _ in range(16)]
        hT = sb("hT", [128, 16, TB], BF16)
        ThT = [Tl() for _ in range(16)]
        catT = sb("catT", [128, 16, TB], BF16)
        Tcat = [Tl() for _ in range(16)]
        reg2 = sb("reg2", [128, 12 * TB], BF16)
        Treg2 = [Tl() for _ in range(12)]
        reg2c = reg2[:, :].rearrange("p (c t) -> p c t", t=TB)
        reg2v = reg2[:, :].rearrange("p (a v) -> p a v", v=1536)
        if hasB:
            ckvT = sb("ckvT", [128, 4, S], BF16)
            TckvT = [[Tl() for _ in range(NB)] for _ in range(4)]
            kropeT = sb("kropeT", [64, S], BF16)
            Tkrope = [Tl() for _ in range(NB)]
        nL = len(ltypes)
        mkb = sb("mkb", [128, 2048], BF16)
        Tmk = Tl()
        qlnb = sb("qlnb", [128, 4, TB], BF16)
        Tqln = [Tl() for _ in range(4)]
        rstd_fin = sb("rstd_fin", [128, TB], F32)
        Trfin = Tl()
        wspb = sb("wspb", [128, 1536], BF16)
        Twsp = Tl()
        NWS = 3
        wsl = [sb("wsl%d" % i, [128, 4096], BF16) for i in range(NWS)]
        Twsl = [Tl() for _ in range(NWS)]
        cstt = sb("cstt", [128, NC_], F32)
        ident = sb("ident", [128, 128], F32)
        trif = sb("trif2", [128, 128], F32)
        trib = sb("trib", [128, 128], BF16)
        ones_bf = sb("ones_bf", [128, 128], BF16)
        on2048 = sb("on2048", [128, 128], BF16)
        on512 = sb("on512", [128, 128], BF16)
        bspt = sb("bspt", [128, 1536], F32)
        Tbsp = Tl()
        halo = sb("halo", [128, nL * 88 * 2], F32)
        Thalo = [Tl() for _ in range(max(nL, 1) * 88)]
        Tk = Tl()
        NTF = 8
        tfs = [sb("tf%d" % i, [128, TB], F32) for i in range(NTF)]
        Ttf = [Tl() for _ in range(NTF)]
        NTBF = 7
        tbs = [sb("tb%d" % i, [128, TB], BF16) for i in range(NTBF)]
        Ttb = [Tl() for _ in range(NTBF)]
        NAE = 4
        aes = [sb("ae%d" % i, [128, TB + 2], F32) for i in range(NAE)]
        Tae = [Tl() for _ in range(NAE)]
        smalls = sb("smalls", [128, 16], F32)
        Tsm = [Tl() for _ in range(4)]
        posi = sb("posi", [64, TB], I32)
        Tposi = Tl()
        cos2 = sb("cos2", [64, TB], F32)
        sin2 = sb("sin2", [64, TB], F32)
        Tcos, Tsin = Tl(), Tl()
        banks = [st.enter_context(nc.psum_tensor("bank%d" % i, [128, TB], F32)) for i in range(8)]
        Tb = [Tl() for _ in range(8)]
        block = st.enter_context(nc.Block())

        rr = {"ps": 0, "tf": 0, "tb": 0, "ae": 0, "w": 0, "ev": 0}
        pinned = set()

        def ps_alloc():
            while True:
                i = rr["ps"] % 8
                rr["ps"] += 1
                if i not in pinned:
                    return i

        def tf_alloc():
            i = rr["tf"] % NTF
            rr["tf"] += 1
            rr["tfi"] = i
            return tfs[i], Ttf[i]

        def tb_alloc():
            i = rr["tb"] % NTBF
            rr["tb"] += 1
            return tbs[i], Ttb[i]

        def ae_alloc():
            i = rr["ae"] % NAE
            rr["ae"] += 1
            return aes[i], Tae[i]

        def cc(i):
            return cstt[:, i:i + 1]

        def mm(out, lhsT, rhs, start, stop, reads, writes):
            P.op("pe", lambda e: e.matmul(out, lhsT, rhs, start=start, stop=stop), reads=reads, writes=writes, acc=True)

        def tr(out, in_, reads, writes):
            P.op("pe", lambda e: e.transpose(out, in_, ident[:]), reads=list(reads) + [Tk], writes=writes, acc=True)

        def act(out, in_, func, reads, writes, bias=None, scale=1.0, accum_out=None):
            kw = {}
            if bias is not None:
                kw["bias"] = bias
            if accum_out is not None:
                kw["accum_out"] = accum_out
            P.op("act", lambda e: e.activation(out, in_, func, scale=scale, **kw), reads=reads, writes=writes)

        def tcopy(eng, out, in_, reads, writes):
            P.op(eng, lambda e: e.tensor_copy(out, in_), reads=reads, writes=writes)

        def evac(out, in_, reads, writes):
            rr["ev"] += 1
            if rr["ev"] % 2 == 0:
                act(out, in_, AF.Copy, reads, writes)
            else:
                tcopy("dve", out, in_, reads, writes)

        def tt(eng, out, in0, in1, op, reads, writes):
            P.op(eng, lambda e: e.tensor_tensor(out=out, in0=in0, in1=in1, op=op), reads=reads, writes=writes)

        def ts(eng, out, in0, s1, s2, op0, op1, reads, writes):
            if s2 is None:
                P.op(eng, lambda e: e.tensor_scalar(out, in0, s1, None, op0=op0), reads=reads, writes=writes)
            else:
                P.op(eng, lambda e: e.tensor_scalar(out, in0, s1, s2, op0=op0, op1=op1), reads=reads, writes=writes)

        def stt(eng, out, in0, scalar, in1, op0, op1, reads, writes):
            P.op(eng, lambda e: e.scalar_tensor_tensor(out=out, in0=in0, scalar=scalar, in1=in1, op0=op0, op1=op1),
                 reads=reads, writes=writes)

        def recip(out, in_, reads, writes):
            P.op("dve", lambda e: e.reciprocal(out, in_), reads=reads, writes=writes)

        def memset(eng, ap, val, writes):
            P.op(eng, lambda e: e.memset(ap, val), writes=writes)

        def dma(q, out, in_, reads, writes, key):
            return P.op(q, lambda e: e.dma_start(out=out, in_=in_), reads=reads, writes=writes, key=key)

        def wget(name, ti):
            nt, kc, ct, _ = cat[name]
            i = rr["w"] % NWS
            rr["w"] += 1
            n = kc * ct
            dma("sp", wsl[i][:, 0:n], scr[name][ti], [], [Twsl[i]], "w%d" % i)
            return wsl[i], Twsl[i]

        dma("pool", cstt[:], dr["cst"][:, :], [], [Tk], "cst")
        dma("pool", ident[:], dr["cmat"][:, 0:128], [], [Tk], "cst")
        dma("pool", trif[:], dr["cmat"][:, 128:256], [], [Tk], "cst")
        tcopy("dve", trib[:], trif[:], [Tk], [Tk])
        memset("dve", ones_bf[:], 1.0, [Tk])
        memset("dve", on2048[:], 1.0 / 2048.0, [Tk])
        memset("dve", on512[:], 1.0 / 512.0, [Tk])
        memset("pool", halo[:], 0.0, Thalo)

        def rms_rstd(src, onm):
            n = len(src)
            N = src[0][0].shape[-1]
            b = ps_alloc()
            for c, (ap, T) in enumerate(src):
                sq, Tsq = tb_alloc()
                act(sq[:, 0:N], ap, AF.Square, [T], [Tsq])
                mm(banks[b][:, 0:N], onm[:], sq[:, 0:N], c == 0, c == n - 1, [Tsq, Tk], [Tb[b]])
            sd, Tsd = tf_alloc()
            act(sd[:, 0:N], banks[b][:, 0:N], AF.Sqrt, [Tb[b], Tk], [Tsd], bias=cc(EPSC))
            recip(sd[:, 0:N], sd[:, 0:N], [Tsd], [Tsd])
            return sd[:, 0:N], Tsd

        def rms_apply(src, dst, g0, rstd, Trstd):
            for c, ((ap, T), (dap, dT)) in enumerate(zip(src, dst)):
                stt("dve", dap, ap, cc(g0 + c), rstd, ALU.mult, ALU.mult, [T, Trstd, Tk], [dT])

        def xchunks():
            return [(xT[:, c, :], TxT[c]) for c in range(16)]

        def hchunks():
            return [(hT[:, c, :], ThT[c]) for c in range(16)]

        def norm_x_to_h(g0):
            rstd, Tr = rms_rstd(xchunks(), on2048)
            rms_apply(xchunks(), hchunks(), g0, rstd, Tr)

        memT = catT[:, :, :].rearrange("p c t -> p (c t)").bitcast(F32).rearrange("p (c m) -> p c m", m=256)
        TmemT = Tcat
        memn = reg2[:, 0:4096].rearrange("p (c m) -> p c m", m=256)
        Tmemn = [Treg2[c // 2] for c in range(16)]
        mk_store = {}
        for mt in range(2):
            for cg in range(4):
                s_, Ts_ = tf_alloc()
                dma("sp", s_[:, :], dr["mem"][mt * 128:(mt + 1) * 128, cg * 512:(cg + 1) * 512], [], [Ts_], "stg%d" % rr["tfi"])
                b = ps_alloc()
                for c in range(4):
                    tr(banks[b][:, c * 128:(c + 1) * 128], s_[:, c * 128:(c + 1) * 128], [Ts_], [Tb[b]])
                evac(memT[:, cg * 4:(cg + 1) * 4, mt * 128:(mt + 1) * 128],
                     banks[b][:, :].rearrange("p (c t) -> p c t", t=128), [Tb[b]], TmemT[cg * 4:(cg + 1) * 4])
        for li, (ty, l, j) in enumerate(ltypes):
            msrc = [(memT[:, c, :], TmemT[c]) for c in range(16)]
            mdst = [(memn[:, c, :], Tmemn[c]) for c in range(16)]
            rstd, Tr = rms_rstd(msrc, on2048)
            rms_apply(msrc, mdst, GMEM + l * 16, rstd, Tr)
            for wi in range(4):
                wt, Tw = wget("mkv%d" % l, wi)
                wv = wt[:, 0:4096].rearrange("p (k c) -> p k c", c=256)
                if wi < 2:
                    for hh in range(2):
                        h = wi * 2 + hh
                        b = ps_alloc()
                        for kc in range(16):
                            mm(banks[b][:, 0:256], wv[:, kc, hh * 128:(hh + 1) * 128], memn[:, kc, :], kc == 0, kc == 15,
                               [Tw, Tmemn[kc]], [Tb[b]])
                        evac(mkb[:, h * 256:(h + 1) * 256], banks[b][:, 0:256], [Tb[b]], [Tmk])
                else:
                    for mt in range(2):
                        b = ps_alloc()
                        for kc in range(16):
                            mm(banks[b][:, 0:256], memn[:, kc, mt * 128:(mt + 1) * 128], wv[:, kc, :], kc == 0, kc == 15,
                               [Tw, Tmemn[kc]], [Tb[b]])
                        c0 = 1024 + mt * 512 + (wi - 2) * 256
                        evac(mkb[:, c0:c0 + 256], banks[b][:, 0:256], [Tb[b]], [Tmk])
            mk_store[li] = dma("sp", scr_mk[li], mkb[:, :], [Tmk], [], "mkst")

        def mem_attention(li):
            sc = 128.0 ** -0.5
            P.op("sp", lambda e: e.dma_start(out=mkb[:, :], in_=scr_mk[li]), writes=[Tmk], key="mkld", extra=[mk_store[li]])
            for h in range(4):
                pts = []
                for mc in range(2):
                    b = ps_alloc()
                    mm(banks[b][:, :], mkb[:, h * 256 + mc * 128:h * 256 + (mc + 1) * 128], reg2c[:, 8 + h, :], True, True,
                       [Tmk, Treg2[8 + h]], [Tb[b]])
                    pt, Tpt = tb_alloc()
                    act(pt[:, :], banks[b][:, :], AF.Exp, [Tb[b]], [Tpt], scale=sc)
                    pts.append((pt, Tpt))
                bo = ps_alloc()
                bd = ps_alloc()
                for mc in range(2):
                    pt, Tpt = pts[mc]
                    mm(banks[bo][:, :], mkb[:, 1024 + mc * 512 + h * 128:1024 + mc * 512 + (h + 1) * 128], pt[:, :], mc == 0, mc == 1,
                       [Tmk, Tpt], [Tb[bo]])
                for mc in range(2):
                    pt, Tpt = pts[mc]
                    mm(banks[bd][:, :], ones_bf[:], pt[:, :], mc == 0, mc == 1, [Tk, Tpt], [Tb[bd]])
                rd, Trd = tf_alloc()
                recip(rd[:, :], banks[bd][:, :], [Tb[bd]], [Trd])
                tt("dve", catT[:, 12 + h, :], banks[bo][:, :], rd[:, :], ALU.mult, [Tb[bo], Trd], [Tcat[12 + h]])

        def out_proj(l):
            for wi in range(8):
                wt, Tw = wget("out%d" % l, wi)
                wv = wt[:, 0:4096].rearrange("p (k c) -> p k c", c=256)
                for hh in range(2):
                    oc = wi * 2 + hh
                    b = ps_alloc()
                    for kc in range(16):
                        mm(banks[b][:, :], wv[:, kc, hh * 128:(hh + 1) * 128], catT[:, kc, :], kc == 0, kc == 15,
                           [Tw, Tcat[kc]], [Tb[b]])
                    tt("dve", xT[:, oc, :], xT[:, oc, :], banks[b][:, :], ALU.add, [Tb[b], TxT[oc]], [TxT[oc]])

        FGROUPS = [(0, 8), (8, 8), (16, 8), (24, 8), (32, 8), (40, 4)]

        def ffn_up(li, l, gi):
            f0, n = FGROUPS[gi]
            gb = (gi % 2) * 8
            for jj in range(n):
                f = f0 + jj
                wt, Tw = wget("up%d" % l, f)
                wv = wt[:, 0:4096].rearrange("p (k c) -> p k c", c=256)
                res = []
                for part in range(2):
                    fc = f + part * NFC
                    b = ps_alloc()
                    for kc in range(16):
                        mm(banks[b][:, :], wv[:, kc, part * 128:(part + 1) * 128], hT[:, kc, :], kc == 0, kc == 15,
                           [Tw, ThT[kc]], [Tb[b]])
                    ae, Ta = ae_alloc()
                    hoff = (li * 88 + fc) * 2
                    Th = Thalo[li * 88 + fc]
                    tcopy("pool", ae[:, 0:2], halo[:, hoff:hoff + 2], [Th], [Ta])
                    act(ae[:, 2:TB + 2], banks[b][:, :], AF.Copy, [Tb[b]], [Ta])
                    tcopy("pool", halo[:, hoff:hoff + 2], ae[:, TB:TB + 2], [Ta], [Th])
                    t1, T1 = tf_alloc()
                    w0 = cc(CW + (l * 3 + 0) * 88 + fc)
                    w1 = cc(CW + (l * 3 + 1) * 88 + fc)
                    w2 = cc(CW + (l * 3 + 2) * 88 + fc)
                    act(t1[:, :], banks[b][:, :], AF.Identity, [Tb[b], Tk], [T1], bias=cc(CB + l * 88 + fc), scale=w2)
                    stt("dve", t1[:, :], ae[:, 1:TB + 1], w1, t1[:, :], ALU.mult, ALU.add, [Ta, Tk, T1], [T1])
                    stt("dve", t1[:, :], ae[:, 0:TB], w0, t1[:, :], ALU.mult, ALU.add, [Ta, Tk, T1], [T1])
                    res.append((t1, T1))
                (tg, Tg), (tv, Tv) = res
                act(tg[:, :], tg[:, :], AF.Silu, [Tg], [Tg])
                tt("dve", catT[:, gb + jj, :], tg[:, :], tv[:, :], ALU.mult, [Tg, Tv], [Tcat[gb + jj]])

        def ffn_down(li, l, gi):
            f0, n = FGROUPS[gi]
            gb = (gi % 2) * 8
            for q in range(4):
                if n == 8:
                    wt, Tw = wget("down%d" % l, gi * 4 + q)
                else:
                    wt, Tw = wget("downt%d" % l, q)
                wv = wt[:, 0:n * 512].rearrange("p (k c) -> p k c", c=512)
                for o4 in range(4):
                    oc = q * 4 + o4
                    b = ps_alloc()
                    for kc in range(n):
                        mm(banks[b][:, :], wv[:, kc, o4 * 128:(o4 + 1) * 128], catT[:, gb + kc, :], kc == 0, kc == n - 1,
                           [Tw, Tcat[gb + kc]], [Tb[b]])
                    tt("dve", xT[:, oc, :], xT[:, oc, :], banks[b][:, :], ALU.add, [Tb[b], TxT[oc]], [TxT[oc]])

        def ffn(li, l):
            norm_x_to_h(GFFN + l * 16)
            ffn_up(li, l, 0)
            for gi in range(len(FGROUPS)):
                if gi + 1 < len(FGROUPS):
                    ffn_up(li, l, gi + 1)
                ffn_down(li, l, gi)

        def mixer_A(li, l, ai):
            norm_x_to_h(GMIX + l * 16)
            for wi in range(6):
                wt, Tw = wget("ina%d" % l, wi)
                wv = wt[:, 0:4096].rearrange("p (k c) -> p k c", c=256)
                for t4 in range(4):
                    b = ps_alloc()
                    for kc in range(16):
                        mm(banks[b][:, 0:256], hT[:, kc, t4 * 128:(t4 + 1) * 128], wv[:, kc, :], kc == 0, kc == 15,
                           [Tw, ThT[kc]], [Tb[b]])
                    act(reg2v[:, t4, wi * 256:(wi + 1) * 256], banks[b][:, 0:256], AF.Gelu, [Tb[b]], Treg2[t4 * 3:t4 * 3 + 3])
            for t4 in range(4):
                Tv3 = Treg2[t4 * 3:t4 * 3 + 3]
                ss = smalls[:, t4:t4 + 1]
                memset("pool", ss, 0.0, [Tsm[t4]])
                act(catT[:, 12:15, :].rearrange("p c t -> p (c t)"), reg2v[:, t4, :], AF.Square, Tv3, Tcat[12:15] + [Tsm[t4]], accum_out=ss)
                act(ss, ss, AF.Sqrt, [Tsm[t4], Tk], [Tsm[t4]], bias=cc(EPSC), scale=1.0 / 1536.0)
                recip(ss, ss, [Tsm[t4]], [Tsm[t4]])
                ts("dve", reg2v[:, t4, :], reg2v[:, t4, :], ss, None, ALU.mult, None, Tv3 + [Tsm[t4]], Tv3)
            dma("sp", wspb[:, :], scr["sp%d" % l][0], [], [Twsp], "wsp")
            dma("sp", bspt[:, :], dr["bsp"][:, ai * 1536:(ai + 1) * 1536], [], [Tbsp], "bsp")
            wsp = wspb
            for wi in range(6):
                wt, Tw = wget("ina%d" % l, 6 + wi)
                wv = wt[:, 0:4096].rearrange("p (k c) -> p k c", c=256)
                for hh in range(2):
                    g = wi * 2 + hh
                    bu = ps_alloc()
                    for kc in range(16):
                        mm(banks[bu][:, :], wv[:, kc, hh * 128:(hh + 1) * 128], hT[:, kc, :], kc == 0, kc == 15,
                           [Tw, ThT[kc]], [Tb[bu]])
                    ug, Tug = tf_alloc()
                    act(ug[:, :], banks[bu][:, :], AF.Gelu, [Tb[bu]], [Tug])
                    bs = ps_alloc()
                    for t4 in range(4):
                        mm(banks[bs][:, t4 * 128:(t4 + 1) * 128], reg2v[:, t4, g * 128:(g + 1) * 128], wsp[:, g * 128:(g + 1) * 128],
                           True, True, Treg2[t4 * 3:t4 * 3 + 3] + [Twsp], [Tb[bs]])
                    sv, Tsv = tf_alloc()
                    for t4 in range(4):
                        stt("dve", sv[:, t4 * 128:(t4 + 1) * 128], banks[bs][:, t4 * 128:(t4 + 1) * 128], cc(GV + l * 12 + g),
                            bspt[:, g * 128:(g + 1) * 128], ALU.mult, ALU.add, [Tb[bs], Tk, Tbsp], [Tsv])
                    tt("dve", catT[:, g, :], sv[:, :], ug[:, :], ALU.mult, [Tsv, Tug], [Tcat[g]])
            for wi in range(2):
                wt, Tw = wget("ina%d" % l, 12 + wi)
                wv = wt[:, 0:4096].rearrange("p (k c) -> p k c", c=256)
                for hh in range(2):
                    h = wi * 2 + hh
                    b = ps_alloc()
                    for kc in range(16):
                        mm(banks[b][:, :], wv[:, kc, hh * 128:(hh + 1) * 128], hT[:, kc, :], kc == 0, kc == 15,
                           [Tw, ThT[kc]], [Tb[b]])
                    evac(reg2c[:, 8 + h, :], banks[b][:, :], [Tb[b]], [Treg2[8 + h]])
            mem_attention(li)
            out_proj(l)

        spbuf = {}

        def wsl_sp(l):
            if l not in spbuf:
                t_ = sb("wsp%d" % l, [128, 1536], BF16)
                T_ = Tl()
                dma("sp", t_[:, :], scr["sp%d" % l][0], [], [T_], "wsp%d" % l)
                spbuf[l] = (t_, T_)
            return spbuf[l]

        def rope_tables(t):
            dma("sp", posi[:, :], dr["posb"][:, t * TB:(t + 1) * TB], [], [Tposi], "posi")
            for which, (dst, Td) in enumerate(((sin2, Tsin), (cos2, Tcos))):
                a, Ta = tf_alloc()
                u, Tu = tf_alloc()
                av, uv = a[0:64, :], u[0:64, :]
                tcopy("dve", av, posi[:, :], [Tposi], [Ta])
                ts("dve", av, av, cstt[0:64, INV:INV + 1], None, ALU.mult, None, [Ta, Tk], [Ta])
                if which == 1:
                    ts("dve", av, av, float(np.pi / 2), None, ALU.add, None, [Ta], [Ta])
                kt_, Tkt = tf_alloc()
                kiv = kt_[0:64, :].bitcast(I32)
                ts("dve", kiv, av, float(1.0 / (2 * np.pi)), None, ALU.mult, None, [Ta], [Tkt])
                tcopy("dve", uv, kiv, [Tkt], [Tu])
                stt("dve", av, uv, float(-C1), av, ALU.mult, ALU.add, [Tu, Ta], [Ta])
                stt("dve", av, uv, float(-C2), av, ALU.mult, ALU.add, [Tu, Ta], [Ta])
                ts("dve", av, av, float(PI_SAFE), float(-PI_SAFE), ALU.min, ALU.max, [Ta], [Ta])
                if which == 0:
                    act(dst[:, :], av, AF.Sin, [Ta, Tk], [Td], scale=cstt[0:64, SGN:SGN + 1])
                else:
                    act(dst[:, :], av, AF.Sin, [Ta], [Td])

        def rope_apply(ba, bb, dst, reads, writes):
            k1, T1 = tf_alloc()
            k2, T2 = tf_alloc()
            tt("dve", k1[0:64, :], ba, cos2[:, :], ALU.mult, reads + [Tcos], [T1])
            tt("dve", k2[0:64, :], bb, sin2[:, :], ALU.mult, reads + [Tsin], [T2])
            tt("dve", dst, k1[0:64, :], k2[0:64, :], ALU.add, [T1, T2], writes)

        def kv_build(t):
            norm_x_to_h(GKV)
            kvf = []
            for wi in range(4):
                wt, Tw = wget("kva", wi)
                wv = wt[:, 0:2048].rearrange("p (k c) -> p k c", c=128)
                b = ps_alloc()
                for kc in range(16):
                    mm(banks[b][:, :], wv[:, kc, :], hT[:, kc, :], kc == 0, kc == 15, [Tw, ThT[kc]], [Tb[b]])
                f_, Tf_ = tf_alloc()
                evac(f_[:, :], banks[b][:, :], [Tb[b]], [Tf_])
                kvf.append((f_[:, :], Tf_))
            wt, Tw = wget("kva", 4)
            wv = wt[:, 0:2048].rearrange("p (k c) -> p k c", c=128)
            ba, bb = ps_alloc(), ps_alloc()
            for kc in range(16):
                mm(banks[ba][0:64, :], wv[:, kc, 0:64], hT[:, kc, :], kc == 0, kc == 15, [Tw, ThT[kc]], [Tb[ba]])
            for kc in range(16):
                mm(banks[bb][0:64, :], wv[:, kc, 64:128], hT[:, kc, :], kc == 0, kc == 15, [Tw, ThT[kc]], [Tb[bb]])
            rope_apply(banks[ba][0:64, :], banks[bb][0:64, :], kropeT[:, t * TB:(t + 1) * TB], [Tb[ba], Tb[bb]], [Tkrope[t]])
            rstd, Tr = rms_rstd(kvf, on512)
            dst = [(ckvT[:, rc, t * TB:(t + 1) * TB], TckvT[rc][t]) for rc in range(4)]
            rms_apply(kvf, dst, GKVL, rstd, Tr)

        def mixer_B(li, l, j, t):
            norm_x_to_h(GMIX + l * 16)
            qlf = []
            for wi in range(4):
                wt, Tw = wget("inb%d" % j, wi)
                wv = wt[:, 0:4096].rearrange("p (k c) -> p k c", c=256)
                for hh in range(2):
                    c = wi * 2 + hh
                    b = ps_alloc()
                    for kc in range(16):
                        mm(banks[b][:, :], wv[:, kc, hh * 128:(hh + 1) * 128], hT[:, kc, :], kc == 0, kc == 15,
                           [Tw, ThT[kc]], [Tb[b]])
                    if c < 4:
                        f_, Tf_ = tf_alloc()
                        evac(f_[:, :], banks[b][:, :], [Tb[b]], [Tf_])
                        qlf.append((f_[:, :], Tf_))
                    else:
                        evac(reg2c[:, 8 + (c - 4), :], banks[b][:, :], [Tb[b]], [Treg2[8 + c - 4]])
            rstd, Tr = rms_rstd(qlf, on512)
            qln = [(qlnb[:, rc, :], Tqln[rc]) for rc in range(4)]
            rms_apply(qlf, qln, GQL + j * 4, rstd, Tr)
            mem_attention(li)
            sc = 192.0 ** -0.5
            for th in range(3):
                wt, Tw = wget("uqn%d" % j, th)
                wv = wt[:, 0:2048].rearrange("p (k c) -> p k c", c=512)
                for hh in range(4):
                    b = ps_alloc()
                    for kc in range(4):
                        mm(banks[b][:, :], wv[:, kc, hh * 128:(hh + 1) * 128], qln[kc][0], kc == 0, kc == 3,
                           [Tw, qln[kc][1]], [Tb[b]])
                    evac(reg2c[:, hh, :], banks[b][:, :], [Tb[b]], [Treg2[hh]])
                wt, Tw = wget("uqr%d" % j, th)
                wv = wt[:, 0:2048].rearrange("p (k c) -> p k c", c=512)
                for hh in range(4):
                    ba, bb = ps_alloc(), ps_alloc()
                    for kc in range(4):
                        mm(banks[ba][0:64, :], wv[:, kc, hh * 128:hh * 128 + 64], qln[kc][0], kc == 0, kc == 3,
                           [Tw, qln[kc][1]], [Tb[ba]])
                    for kc in range(4):
                        mm(banks[bb][0:64, :], wv[:, kc, hh * 128 + 64:hh * 128 + 128], qln[kc][0], kc == 0, kc == 3,
                           [Tw, qln[kc][1]], [Tb[bb]])
                    rope_apply(banks[ba][0:64, :], banks[bb][0:64, :], reg2c[0:64, 4 + hh, :], [Tb[ba], Tb[bb]], [Treg2[4 + hh]])
                wk, Twk = wget("uk%d" % j, th)
                wkv = wk[:, 0:2048].rearrange("p (k c) -> p k c", c=512)
                wu, Twu = wget("uv%d" % j, th)
                wuv = wu[:, 0:2048].rearrange("p (k c) -> p k c", c=512)
                for hh in range(4):
                    h = th * 4 + hh
                    bacc, bden = ps_alloc(), ps_alloc()
                    pinned.add(bacc)
                    pinned.add(bden)
                    first = True
                    for kg in range(t + 1):
                        b = ps_alloc()
                        for rc in range(4):
                            mm(banks[b][:, :], wkv[:, rc, hh * 128:(hh + 1) * 128], ckvT[:, rc, kg * TB:(kg + 1) * TB], rc == 0, rc == 3,
                               [Twk, TckvT[rc][kg]], [Tb[b]])
                        kh, Tkh = tb_alloc()
                        evac(kh[:, :], banks[b][:, :], [Tb[b]], [Tkh])
                        b = ps_alloc()
                        for jx in range(4):
                            for rc in range(4):
                                mm(banks[b][:, jx * 128:(jx + 1) * 128], ckvT[:, rc, kg * TB + jx * 128:kg * TB + (jx + 1) * 128],
                                   wuv[:, rc, hh * 128:(hh + 1) * 128], rc == 0, rc == 3, [Twu, TckvT[rc][kg]], [Tb[b]])
                        vh, Tvh = tb_alloc()
                        evac(vh[:, :], banks[b][:, :], [Tb[b]], [Tvh])
                        def s_part(jx):
                            kt = kg * 4 + jx
                            c0 = jx * 128 if kg == t else 0
                            b = ps_alloc()
                            mm(banks[b][:, c0:TB], kh[:, jx * 128:(jx + 1) * 128], reg2c[:, hh, c0:TB], True, False,
                               [Tkh, Treg2[hh]], [Tb[b]])
                            mm(banks[b][:, c0:TB], kropeT[:, kt * 128:(kt + 1) * 128], reg2c[0:64, 4 + hh, c0:TB], False, True,
                               [Tkrope[kg], Treg2[4 + hh]], [Tb[b]])
                            pt, Tpt = tb_alloc()
                            act(pt[:, c0:TB], banks[b][:, c0:TB], AF.Exp, [Tb[b]], [Tpt], scale=sc)
                            if kg == t:
                                tt("pool", pt[:, c0:c0 + 128], pt[:, c0:c0 + 128], trib[:], ALU.mult, [Tpt, Tk], [Tpt])
                            return (jx, c0, pt, Tpt)

                        def pv_part(info, first):
                            jx, c0, pt, Tpt = info
                            mm(banks[bacc][:, c0:TB], vh[:, jx * 128:(jx + 1) * 128], pt[:, c0:TB], first, False, [Tvh, Tpt], [Tb[bacc]])
                            mm(banks[bden][:, c0:TB], ones_bf[:], pt[:, c0:TB], first, False, [Tk, Tpt], [Tb[bden]])

                        infos = {0: s_part(0), 1: s_part(1)}
                        for jx in range(4):
                            pv_part(infos[jx], first)
                            first = False
                            if jx + 2 < 4:
                                infos[jx + 2] = s_part(jx + 2)
                    rd, Trd = tf_alloc()
                    recip(rd[:, :], banks[bden][:, :], [Tb[bden]], [Trd])
                    tt("dve", catT[:, h, :], banks[bacc][:, :], rd[:, :], ALU.mult, [Tb[bacc], Trd], [Tcat[h]])
                    pinned.discard(bacc)
                    pinned.discard(bden)
            out_proj(l)

        final_stores = []
        for t in range(NB):
            for t4 in range(4):
                for cg in range(4):
                    s_, Ts_ = tf_alloc()
                    r0 = t * TB + t4 * 128
                    dma("sp", s_[:, :], dr["x"][r0:r0 + 128, cg * 512:(cg + 1) * 512], [], [Ts_], "stg%d" % rr["tfi"])
                    b = ps_alloc()
                    for c in range(4):
                        tr(banks[b][:, c * 128:(c + 1) * 128], s_[:, c * 128:(c + 1) * 128], [Ts_], [Tb[b]])
                    evac(xT[:, cg * 4:(cg + 1) * 4, t4 * 128:(t4 + 1) * 128],
                         banks[b][:, :].rearrange("p (c t) -> p c t", t=128), [Tb[b]], TxT[cg * 4:(cg + 1) * 4])
            if hasB:
                rope_tables(t)
            ai = 0
            kv_done = False
            for li, (ty, l, j) in enumerate(ltypes):
                if ty == "A":
                    mixer_A(li, l, ai)
                    ai += 1
                else:
                    if not kv_done:
                        kv_build(t)
                        kv_done = True
                    mixer_B(li, l, j, t)
                ffn(li, l)
            rstd0, Tr0 = rms_rstd(xchunks(), on2048)
            tcopy("dve", rstd_fin[:, :], rstd0, [Tr0], [Trfin])
            rstd, Tr = rstd_fin[:, :], Trfin
            for cg in range(4):
                ys = []
                for c4 in range(4):
                    c = cg * 4 + c4
                    y_, Ty_ = tf_alloc()
                    stt("dve", y_[:, :], xT[:, c, :], cc(GFIN + c), rstd, ALU.mult, ALU.mult, [TxT[c], Tr, Tk], [Ty_])
                    ys.append((y_, Ty_))
                for t4 in range(4):
                    b = ps_alloc()
                    for c4 in range(4):
                        tr(banks[b][:, c4 * 128:(c4 + 1) * 128], ys[c4][0][:, t4 * 128:(t4 + 1) * 128], [ys[c4][1]], [Tb[b]])
                    o_, To_ = tf_alloc()
                    evac(o_[:, :], banks[b][:, :], [Tb[b]], [To_])
                    r0 = t * TB + t4 * 128
                    final_stores.append(dma("pool", y[r0:r0 + 128, cg * 512:(cg + 1) * 512], o_[:, :], [To_], [],
                                            "ost%d" % rr["tfi"]))
        lastk = {}
        for o in final_stores:
            lastk[o.key] = o
        P.wait("pool", list(lastk.values()))
        P.emit(block)
    return nc


LT_FULL = [("A", 0, 0), ("A", 1, 1), ("B", 2, 0), ("B", 3, 1)]


def host_inputs(inp, ltypes, S, b):
    f = lambda a: np.ascontiguousarray(np.asarray(a, dtype=np.float32))
    m = {}
    m["x"] = f(inp["x"][b, :S])
    m["mem"] = f(inp["mem"][b])
    m["posb"] = np.ascontiguousarray(np.broadcast_to(np.asarray(inp["positions"][b, :S], dtype=np.int32)[None, :], (64, S)))
    cst = np.zeros((128, NC_), np.float32)

    def pm(v):
        return np.asarray(v, np.float32).reshape(-1, 128).T

    for l in range(4):
        cst[:, GMIX + l * 16:GMIX + (l + 1) * 16] = pm(inp["g_mix"][l])
        cst[:, GFFN + l * 16:GFFN + (l + 1) * 16] = pm(inp["g_ffn"][l])
        cst[:, GMEM + l * 16:GMEM + (l + 1) * 16] = pm(inp["g_mem"][l])
        for k in range(3):
            cst[:, CW + (l * 3 + k) * 88:CW + (l * 3 + k + 1) * 88] = pm(inp["conv_w"][l, k])
        cst[:, CB + l * 88:CB + (l + 1) * 88] = pm(inp["conv_b"][l])
    cst[:, GFIN:GFIN + 16] = pm(inp["g_final"])
    cst[:, GKV:GKV + 16] = pm(inp["g_kv"])
    cst[:, GKVL:GKVL + 4] = pm(inp["g_kv_lat"])
    for j in range(2):
        cst[:, GQL + j * 4:GQL + (j + 1) * 4] = pm(inp["g_q_lat"][j])
        cst[:, GV + j * 12:GV + (j + 1) * 12] = pm(inp["g_v"][j])
    inv = (1.0 / (10000.0 ** (np.arange(0, 64, 2, dtype=np.float32) / np.float32(64)))).astype(np.float32)
    cst[0:32, INV] = inv
    cst[32:64, INV] = inv
    cst[0:32, SGN] = -1.0
    cst[32:64, SGN] = 1.0
    cst[:, EPSC] = EPS
    m["cst"] = cst
    cm = np.zeros((128, 256), np.float32)
    cm[:, 0:128] = np.eye(128, dtype=np.float32)
    cm[:, 128:256] = np.triu(np.ones((128, 128), np.float32))
    m["cmat"] = cm
    nA = sum(1 for t in ltypes if t[0] == "A")
    bsp = np.zeros((128, max(nA, 1) * 1536), np.float32)
    ai = 0
    for (ty, l, j) in ltypes:
        if ty == "A":
            bsp[:, ai * 1536:(ai + 1) * 1536] = np.asarray(inp["b_sp"][l], np.float32).reshape(1, 1536)
            ai += 1
            m["w_in_a%d" % l] = f(inp["w_in_a"][l])
            m["w_spt%d" % l] = f(np.transpose(np.asarray(inp["w_sp"][l]), (2, 0, 1)).reshape(128, 1536))
        else:
            m["w_in_b%d" % j] = f(inp["w_in_b"][j])
            wq = np.asarray(inp["w_uq"][j], np.float32).reshape(512, 12, 192)
            m["w_uqn%d" % j] = f(wq[:, :, 0:128].reshape(512, 1536))
            m["w_uqr%d" % j] = f(np.concatenate([wq[:, :, 128:192], wq[:, :, 160:192], wq[:, :, 128:160]], axis=2).reshape(512, 1536))
            m["w_uk%d" % j] = f(np.asarray(inp["w_uk"][j]).reshape(512, 1536))
            m["w_uv%d" % j] = f(np.asarray(inp["w_uv"][j]).reshape(512, 1536))
        m["w_out%d" % l] = f(inp["w_out"][l])
        m["w_up%d" % l] = f(inp["w_ffn_up"][l])
        m["w_down%d" % l] = f(inp["w_ffn_down"][l])
        m["w_mkv%d" % l] = f(inp["w_mem_kv"][l])
    m["bsp"] = bsp
    if any(t[0] == "B" for t in ltypes):
        wk = np.asarray(inp["w_kv_a"], np.float32)
        m["w_kva"] = f(np.concatenate([wk[:, 0:576], wk[:, 544:576], wk[:, 512:544]], axis=1))
    return m


_CACHE = {}


def run(inp, ltypes, S, ncores=8):
    key = (S, tuple(ltypes))
    if key not in _CACHE:
        _CACHE[key] = build(S, ltypes)
    nc = _CACHE[key]
    shared = None
    in_maps = []
    for b in range(ncores):
        mb = host_inputs(inp, ltypes, S, b) if shared is None else dict(shared)
        if shared is None:
            shared = mb
        else:
            f = lambda a: np.ascontiguousarray(np.asarray(a, dtype=np.float32))
            mb["x"] = f(inp["x"][b, :S])
            mb["mem"] = f(inp["mem"][b])
            mb["posb"] = np.ascontiguousarray(np.broadcast_to(np.asarray(inp["positions"][b, :S], dtype=np.int32)[None, :], (64, S)))
        in_maps.append(mb)
    res = run_bass_kernel_spmd(nc, in_maps, core_ids=list(range(ncores)))
    return np.stack([np.asarray(r["y"], dtype=np.float32) for r in res.results], axis=0)


def kernel(**inputs):
    return run(inputs, LT_FULL, 4096, 8)
```
